# Optimizing a Trainium2 kernel written in Bass

```python
import jax, jax.numpy as jnp
from jax import lax
import numpy as np

D_MODEL = 1024
BATCH = 32
SEQ = 2048
DEPTH = 1

CHUNK = 64
EPS = 1e-6
D_FF = 4 * D_MODEL
SGU_BLOCK = 128
SGU_GROUPS = 8
SGU_WIDTH = D_MODEL
SGU_GDIM = SGU_WIDTH // SGU_GROUPS
SSM_INNER = 2 * D_MODEL
SSM_HEADDIM = 64
SSM_HEADS = SSM_INNER // SSM_HEADDIM
SSM_GROUPS = 8
SSM_HPG = SSM_HEADS // SSM_GROUPS
SSM_STATE = 128
SSM_CONV = 4
SSM_CONV_DIM = SSM_INNER + 2 * SSM_GROUPS * SSM_STATE
SSM_CHUNK = CHUNK
IN_WIDTHS = (SGU_WIDTH, SGU_WIDTH, SSM_INNER, SSM_CONV_DIM, SSM_HEADS, D_MODEL, D_MODEL)
IN_DIM = sum(IN_WIDTHS)
IN_SPLITS = [int(s) for s in np.cumsum(IN_WIDTHS)[:-1]]

kernel_name = "hybrid_gmlp_ssd_macaron_block"


def rmsnorm(x, g):
    xf = x.astype(jnp.float32)
    y = xf * lax.rsqrt(jnp.mean(xf * xf, axis=-1, keepdims=True) + EPS)
    return (y * g.astype(jnp.float32)).astype(x.dtype)


def layernorm(x, g, b):
    xf = x.astype(jnp.float32)
    mu = jnp.mean(xf, axis=-1, keepdims=True)
    var = jnp.mean(jnp.square(xf - mu), axis=-1, keepdims=True)
    y = (xf - mu) * lax.rsqrt(var + EPS)
    return (y * g.astype(jnp.float32) + b.astype(jnp.float32)).astype(x.dtype)


def swiglu(x, w_in, w_out):
    gate, up = jnp.split(x @ w_in, 2, axis=-1)
    return (jax.nn.silu(gate) * up) @ w_out


def sgu_branch(u, v, ln_g, ln_b, w_s, b_s):
    bsz, L, _ = v.shape
    nblk = L // SGU_BLOCK
    v = layernorm(v, ln_g, ln_b).reshape(bsz, nblk, SGU_BLOCK, SGU_GROUPS, SGU_GDIM)
    cidx = np.arange(SGU_BLOCK) // CHUNK
    mask = (cidx[None, :] <= cidx[:, None])
    w = w_s * jnp.asarray(mask, w_s.dtype)[None]
    f = jnp.einsum("gts,bnsgc->bntgc", w, v) + b_s.T[None, None, :, :, None]
    return u * f.reshape(bsz, L, SGU_WIDTH)


def ssd_scan(xdt, adt, bm, cm):
    bsz, L = xdt.shape[:2]
    nc = L // SSM_CHUNK
    Q = SSM_CHUNK
    x = xdt.reshape(bsz, nc, Q, SSM_GROUPS, SSM_HPG, SSM_HEADDIM)
    a = adt.reshape(bsz, nc, Q, SSM_GROUPS, SSM_HPG)
    b = bm.reshape(bsz, nc, Q, SSM_GROUPS, SSM_STATE)
    c = cm.reshape(bsz, nc, Q, SSM_GROUPS, SSM_STATE)
    a_cs = jnp.cumsum(a, axis=2)
    tri = np.tril(np.ones((Q, Q), dtype=bool))[:, :, None, None]
    seg = a_cs[:, :, :, None] - a_cs[:, :, None, :]
    decay = jnp.exp(jnp.where(tri, seg, -jnp.inf))
    cb = jnp.einsum("bctgn,bcsgn->bctsg", c, b)
    y_diag = jnp.einsum("bctsg,bctsgk,bcsgkp->bctgkp", cb, decay, x)
    decay_to_end = jnp.exp(a_cs[:, :, -1:] - a_cs)
    states = jnp.einsum("bclgn,bclgk,bclgkp->bcgkpn", b, decay_to_end, x)
    chunk_decay = jnp.exp(a_cs[:, :, -1])

    def step(s, inp):
        st, dec = inp
        return s * dec[..., None, None] + st, s

    s0 = jnp.zeros((bsz, SSM_GROUPS, SSM_HPG, SSM_HEADDIM, SSM_STATE), x.dtype)
    _, prev = lax.scan(step, s0, (jnp.moveaxis(states, 1, 0), jnp.moveaxis(chunk_decay, 1, 0)))
    prev = jnp.moveaxis(prev, 0, 1)
    y_off = jnp.einsum("bctgn,bcgkpn,bctgk->bctgkp", c, prev, jnp.exp(a_cs))
    return (y_diag + y_off).reshape(bsz, L, SSM_HEADS, SSM_HEADDIM)


def mamba2_branch(z, xbc, dt_raw, conv_w, conv_b, dt_bias, a_log, d_skip, norm_g):
    bsz, L, _ = xbc.shape
    xbc = lax.conv_general_dilated(
        xbc, conv_w[:, None, :], window_strides=(1,), padding=[(SSM_CONV - 1, 0)],
        dimension_numbers=("NWC", "WIO", "NWC"), feature_group_count=SSM_CONV_DIM) + conv_b
    xbc = jax.nn.silu(xbc)
    xs, bm, cm = jnp.split(xbc, [SSM_INNER, SSM_INNER + SSM_GROUPS * SSM_STATE], axis=-1)
    xs = xs.reshape(bsz, L, SSM_HEADS, SSM_HEADDIM)
    bm = bm.reshape(bsz, L, SSM_GROUPS, SSM_STATE).astype(jnp.float32)
    cm = cm.reshape(bsz, L, SSM_GROUPS, SSM_STATE).astype(jnp.float32)
    dt = jax.nn.softplus(dt_raw.astype(jnp.float32) + dt_bias.astype(jnp.float32))
    a = -jnp.exp(a_log.astype(jnp.float32))
    y = ssd_scan(xs.astype(jnp.float32) * dt[..., None], dt * a, bm, cm)
    y = y + xs.astype(jnp.float32) * d_skip.astype(jnp.float32)[:, None]
    yg = (y.reshape(bsz, L, SSM_INNER) * jax.nn.silu(z.astype(jnp.float32)))
    yg = yg.reshape(bsz, L, SSM_GROUPS, SSM_INNER // SSM_GROUPS)
    yg = yg * lax.rsqrt(jnp.mean(yg * yg, axis=-1, keepdims=True) + EPS)
    return (yg.reshape(bsz, L, SSM_INNER) * norm_g.astype(jnp.float32)).astype(z.dtype)


def setup_inputs(seed: int = 0) -> dict:
    key = jax.random.key(seed)
    ks = jax.random.split(key, 24)
    f32 = jnp.float32
    nrm = lambda k, shape, s: jax.random.normal(k, shape, f32) * s
    gain = lambda k, shape: 1.0 + 0.02 * jax.random.normal(k, shape, f32)
    dt = jnp.exp(jax.random.uniform(ks[13], (DEPTH, SSM_HEADS), f32, np.log(1e-3), np.log(1e-1)))
    return {
        "x": jax.random.normal(ks[0], (BATCH, SEQ, D_MODEL), f32),
        "ffn1_norm": gain(ks[1], (DEPTH, D_MODEL)),
        "ffn1_w_in": nrm(ks[2], (DEPTH, D_MODEL, 2 * D_FF), D_MODEL ** -0.5),
        "ffn1_w_out": nrm(ks[3], (DEPTH, D_FF, D_MODEL), D_FF ** -0.5),
        "mix_norm": gain(ks[4], (DEPTH, D_MODEL)),
        "w_in": nrm(ks[5], (DEPTH, D_MODEL, IN_DIM), D_MODEL ** -0.5),
        "sgu_ln_g": gain(ks[6], (DEPTH, SGU_WIDTH)),
        "sgu_ln_b": nrm(ks[7], (DEPTH, SGU_WIDTH), 0.02),
        "sgu_w_s": nrm(ks[8], (DEPTH, SGU_GROUPS, SGU_BLOCK, SGU_BLOCK), SGU_BLOCK ** -0.5),
        "sgu_b_s": gain(ks[9], (DEPTH, SGU_GROUPS, SGU_BLOCK)),
        "conv_w": nrm(ks[10], (DEPTH, SSM_CONV, SSM_CONV_DIM), SSM_CONV ** -0.5),
        "conv_b": nrm(ks[11], (DEPTH, SSM_CONV_DIM), 0.02),
        "dt_bias": dt + jnp.log(-jnp.expm1(-dt)),
        "a_log": jnp.log(jax.random.uniform(ks[12], (DEPTH, SSM_HEADS), f32, 1.0, 16.0)),
        "d_skip": gain(ks[14], (DEPTH, SSM_HEADS)),
        "ssm_norm": gain(ks[15], (DEPTH, SSM_INNER)),
        "w_a": nrm(ks[16], (DEPTH, SGU_WIDTH, D_MODEL), SGU_WIDTH ** -0.5),
        "w_b": nrm(ks[17], (DEPTH, SSM_INNER, D_MODEL), SSM_INNER ** -0.5),
        "w_o": nrm(ks[18], (DEPTH, D_MODEL, D_MODEL), D_MODEL ** -0.5),
        "ffn2_norm": gain(ks[19], (DEPTH, D_MODEL)),
        "ffn2_w_in": nrm(ks[20], (DEPTH, D_MODEL, 2 * D_FF), D_MODEL ** -0.5),
        "ffn2_w_out": nrm(ks[21], (DEPTH, D_FF, D_MODEL), D_FF ** -0.5),
        "final_norm": gain(ks[22], (D_MODEL,)),
    }


def reference(x, ffn1_norm, ffn1_w_in, ffn1_w_out, mix_norm, w_in, sgu_ln_g, sgu_ln_b,
              sgu_w_s, sgu_b_s, conv_w, conv_b, dt_bias, a_log, d_skip, ssm_norm,
              w_a, w_b, w_o, ffn2_norm, ffn2_w_in, ffn2_w_out, final_norm):
    h = x
    for l in range(DEPTH):
        h = h + 0.5 * swiglu(rmsnorm(h, ffn1_norm[l]), ffn1_w_in[l], ffn1_w_out[l])
        n = rmsnorm(h, mix_norm[l])
        u, v, z, xbc, dt_raw, g_a, g_b = jnp.split(n @ w_in[l], IN_SPLITS, axis=-1)
        y_a = sgu_branch(jax.nn.gelu(u, approximate=False), jax.nn.gelu(v, approximate=False),
                         sgu_ln_g[l], sgu_ln_b[l], sgu_w_s[l], sgu_b_s[l]) @ w_a[l]
        y_b = mamba2_branch(z, xbc, dt_raw, conv_w[l], conv_b[l], dt_bias[l], a_log[l],
                            d_skip[l], ssm_norm[l]) @ w_b[l]
        merged = jax.nn.sigmoid(g_a) * y_a + jax.nn.sigmoid(g_b) * y_b
        h = h + merged @ w_o[l]
        h = h + 0.5 * swiglu(rmsnorm(h, ffn2_norm[l]), ffn2_w_in[l], ffn2_w_out[l])
    return rmsnorm(h, final_norm)
```

```python
import numpy as np
import concourse.bass as bass
import concourse.mybir as mybir

F32 = mybir.dt.float32
BF16 = mybir.dt.bfloat16
AF = mybir.ActivationFunctionType
ALU = mybir.AluOpType
AX = mybir.AxisListType

PAGE = 256
_ESZ = {F32: 4, BF16: 2}


class Buf:
    __slots__ = ("last_w", "readers", "excl")

    def __init__(self, excl=False):
        self.last_w = None
        self.readers = []
        self.excl = excl


class Acc:
    __slots__ = ("ap", "bufs")

    def __init__(self, ap, bufs):
        self.ap = ap
        self.bufs = bufs


class Tn:
    def __init__(self, handle, space, off, shape, dtype, pages):
        self.t = handle
        self.space = space
        self.off = off
        self.shape = list(shape)
        self.dtype = dtype
        self.esz = _ESZ[dtype]
        self.pages = pages
        st = []
        s = 1
        for d in reversed(self.shape[1:]):
            st.append(s)
            s *= d
        self.strides = list(reversed(st))

    def _ranges(self, idx):
        dims = self.shape[1:]
        sel = []
        for i, d in enumerate(dims):
            if i < len(idx):
                k = idx[i]
                if isinstance(k, int):
                    sel.append((k, k + 1))
                else:
                    a = 0 if k.start is None else k.start
                    b = d if k.stop is None else k.stop
                    assert k.step in (None, 1)
                    sel.append((a, b))
            else:
                sel.append((0, d))
        n = len(dims)
        runs = [(0, 1)]
        inner = 1
        j = n - 1
        while j >= 0 and sel[j] == (0, dims[j]):
            inner *= dims[j]
            j -= 1
        if j < 0:
            return [(0, inner)]
        a, b = sel[j]
        base_runs = [(a * self.strides[j], (b - a) * inner)]
        for jj in range(j - 1, -1, -1):
            a, b = sel[jj]
            new = []
            for k in range(a, b):
                for (o, l) in base_runs:
                    new.append((o + k * self.strides[jj], l))
            base_runs = new
        return base_runs

    def __getitem__(self, idx):
        if not isinstance(idx, tuple):
            idx = (idx,)
        ap = self.t[idx]
        bufs = []
        seen = set()
        for (o, l) in self._ranges(idx[1:]):
            lo = self.off + o * self.esz
            hi = self.off + (o + l) * self.esz
            for p in range(lo // PAGE, (hi - 1) // PAGE + 1):
                if p not in seen:
                    seen.add(p)
                    b = self.pages.get(p)
                    if b is None:
                        b = self.pages[p] = Buf()
                    bufs.append(b)
        return Acc(ap, bufs)


class Mem:
    def __init__(self, nc):
        self.nc = nc
        self.base = (nc.sbuf_base + 255) // 256 * 256
        self.top = nc.sbuf_top
        self.cur = self.base
        self.sb_pages = {}
        self.ps_pages = {}
        self.n = 0
        self.ps = nc.alloc_psum_tensor("psall", [128, 4096], F32)

    def at(self, name, shape, dtype, off):
        self.n += 1
        size = int(np.prod(shape[1:])) * _ESZ[dtype]
        assert self.base + off + size <= self.top, (name, off, size, self.top - self.base)
        h = self.nc.alloc_sbuf_tensor_at("%s_%d" % (name, self.n), list(shape), dtype, offset=self.base + off)
        return Tn(h, "sb", self.base + off, shape, dtype, self.sb_pages)

    def alloc(self, name, shape, dtype):
        size = int(np.prod(shape[1:])) * _ESZ[dtype]
        off = self.cur - self.base
        self.cur += (size + 255) // 256 * 256
        assert self.cur <= self.top, ("SBUF overflow", name, self.cur - self.base, self.top - self.base)
        return self.at(name, shape, dtype, off)

    def used(self):
        return self.cur - self.base

    def psum(self, bank, nbanks=1, dtype=F32, shape=None):
        ap = self.ps[:, bank * 512:(bank + nbanks) * 512]
        n = nbanks * 512
        if dtype != F32:
            ap = ap.bitcast(dtype)
            n = n * 4 // _ESZ[dtype]
        return PsView(ap, bank, nbanks, dtype, n, self.ps_pages, shape)


class PsView:
    def __init__(self, ap, bank, nbanks, dtype, n, pages, shape):
        self.ap0 = ap
        self.bufs = []
        for b in range(bank, bank + nbanks):
            if b not in pages:
                pages[b] = Buf(excl=True)
            self.bufs.append(pages[b])
        if shape is not None:
            names = " ".join("d%d" % i for i in range(len(shape)))
            kw = {"d%d" % i: s for i, s in enumerate(shape)}
            self.ap0 = ap.rearrange("p (%s) -> p %s" % (names, names), **kw)

    def __getitem__(self, idx):
        return Acc(self.ap0[idx], self.bufs)


class DramBuf:
    def __init__(self):
        self.buf = Buf()

    def acc(self, ap):
        return Acc(ap, [self.buf])


class _Stop(Exception):
    pass


class Ins:
    __slots__ = ("eng", "fn", "deps", "is_dma", "sem_key", "sig", "sig_val")

    def __init__(self, eng, fn, is_dma, sem_key):
        self.eng = eng
        self.fn = fn
        self.deps = ()
        self.is_dma = is_dma
        self.sem_key = sem_key
        self.sig = False
        self.sig_val = 0


class Sched:
    ENGS = ("pe", "act", "dve", "pool", "sp")

    def __init__(self, same_eng_raw=True):
        self.streams = {e: [] for e in self.ENGS}
        self.same_eng_raw = same_eng_raw

    nops = 0
    limit = None

    def op(self, eng, fn, reads=(), writes=(), dma=None, acc=False):
        Sched.nops += 1
        if Sched.limit is not None and Sched.nops > Sched.limit:
            raise _Stop()
        ins = Ins(eng, fn, dma is not None, dma)
        is_dma = dma is not None
        deps = set()
        for a in reads:
            for b in a.bufs:
                w = b.last_w
                if w is not None:
                    deps.add(w)
                if b.excl:
                    for r in b.readers:
                        if r.eng != eng:
                            deps.add(r)
        for a in writes:
            for b in a.bufs:
                w = b.last_w
                if w is not None:
                    if not (eng == "pe" and w.eng == "pe"):
                        deps.add(w)
                for r in b.readers:
                    if r is not ins:
                        deps.add(r)
        deps.discard(ins)
        ins.deps = tuple(deps)
        for a in writes:
            for b in a.bufs:
                b.last_w = ins
                b.readers = []
        for a in reads:
            for b in a.bufs:
                b.readers.append(ins)
        self.streams[eng].append(ins)
        return ins

    def emit(self, nc):
        for e in self.ENGS:
            for ins in self.streams[e]:
                if ins.is_dma:
                    ins.sig = True
                for d in ins.deps:
                    d.sig = True
        cnt = {}
        for e in self.ENGS:
            for ins in self.streams[e]:
                if ins.sig:
                    key = ("dma", ins.sem_key) if ins.is_dma else ("eng", e)
                    ins.sem_key = key
                    cnt[key] = cnt.get(key, 0) + (16 if ins.is_dma else 1)
                    ins.sig_val = cnt[key]
        sems = {}
        for k in sorted(cnt):
            sems[k] = nc.alloc_semaphore(name=("s_%s_%s" % k).replace(".", "_"))
        self.maxcnt = dict(cnt)
        nwaits = [0]
        with nc.Block() as block:
            def run(e):
                def body(eng):
                    waited = {}
                    for ins in self.streams[e]:
                        need = {}
                        for d in ins.deps:
                            k = d.sem_key
                            if waited.get(k, 0) >= d.sig_val:
                                continue
                            if need.get(k, 0) < d.sig_val:
                                need[k] = d.sig_val
                        for k, v in need.items():
                            eng.wait_ge(sems[k], v)
                            waited[k] = v
                            nwaits[0] += 1
                        bi = ins.fn(eng)
                        if ins.sig:
                            bi.then_inc(sems[ins.sem_key], 16 if ins.is_dma else 1)
                return body
            block.tensor(run("pe"))
            block.scalar(run("act"))
            block.vector(run("dve"))
            block.gpsimd(run("pool"))
            block.sync(run("sp"))
        self.stats = dict(waits=nwaits[0], n={e: len(self.streams[e]) for e in self.ENGS}, sems=len(sems),
                          maxcnt=max(cnt.values()) if cnt else 0)


D = 1024
KC = 8
TT = 512
NBLK = 4
FF = 4096
EPS = 1e-6
RING = 4
UNIT_ELEMS = 4096

C_U, C_V, C_Z, C_X, C_DT, C_GA, C_GB = 0, 1024, 2048, 4096, 8192, 8224, 9248

P_FFN1, P_MIX, P_FFN2, P_FIN, P_CW, P_CB, P_SSMG, P_LNG, P_LNB = 0, 8, 16, 24, 32, 160, 192, 208, 216
NCPP = 224
B_DTB, B_ALOG, B_DSK, B_BS = 0, 32, 64, 96
NBCS = 96 + 1024


def unit_list():
    L = []

    def ffn(pfx):
        for i in range(8):
            L.append((pfx + "_w_in", 8, 512 * i, 512))
            L.append((pfx + "_w_in", 8, FF + 512 * i, 512))
        for m in range(8):
            L.append((pfx + "_w_out", 32, 128 * m, 128))
    ffn("ffn1")
    for i in range(2):
        L.append(("w_in", 8, C_U + 512 * i, 512))
    for i in range(2):
        L.append(("w_in", 8, C_V + 512 * i, 512))
    for i in range(4):
        L.append(("w_in", 8, C_Z + 512 * i, 512))
    for i in range(2):
        L.append(("w_in", 8, C_GA + 512 * i, 512))
        L.append(("w_a", 8, 512 * i, 512))
    for i in range(8):
        L.append(("w_in", 8, C_X + 512 * i, 512))
    L.append(("w_in", 8, C_DT, 32))
    for i in range(2):
        L.append(("w_in", 8, C_GB + 512 * i, 512))
        L.append(("w_b", 16, 512 * i, 256))
        L.append(("w_b", 16, 512 * i + 256, 256))
    for i in range(2):
        L.append(("w_o", 8, 512 * i, 512))
    ffn("ffn2")
    return L


W_SHAPES = {"ffn1_w_in": (D, 2 * FF), "ffn1_w_out": (FF, D), "w_in": (D, 10272), "w_a": (D, D), "w_b": (2 * D, D),
            "w_o": (D, D), "ffn2_w_in": (D, 2 * FF), "ffn2_w_out": (FF, D)}


DEBUG_STOP = [None]
DEBUG_BLK = [0]
DEBUG_SUB = [None]


def build_nc(n_seq, seq_len):
    def chk(k):
        if DEBUG_STOP[0] == k:
            raise _Stop()
    ntok = n_seq * seq_len
    tiles_per_seq = seq_len // TT
    ntiles = n_seq * tiles_per_seq
    nc = bass.Bass("TRN2", target_bir_lowering=False)
    x_d = nc.dram_tensor("x", [ntok, D], F32, kind="ExternalInput").ap()
    out_d = nc.dram_tensor("out", [ntok, D], F32, kind="ExternalOutput").ap()
    W = {k: nc.dram_tensor(k, list(s), F32, kind="ExternalInput").ap() for k, s in W_SHAPES.items()}
    cpp_d = nc.dram_tensor("cpp", [128, NCPP], F32, kind="ExternalInput").ap()
    bcs_d = nc.dram_tensor("bcs", [128, NBCS], F32, kind="ExternalInput").ap()
    wst_d = nc.dram_tensor("wst", [128, 8, 128], F32, kind="ExternalInput").ap()
    units = unit_list()
    NU = len(units)
    wscr = nc.dram_tensor("wscr", [NU, 128, UNIT_ELEMS], BF16).ap()
    wscr_b = [DramBuf() for _ in range(NU)]
    out_b = [DramBuf(), DramBuf()]

    M = Mem(nc)
    S = Sched()
    ops = S.op

    cpp = M.alloc("cpp", [128, NCPP], F32)
    bcs = M.alloc("bcs", [128, 96], F32)
    Kc = M.alloc("Kc", [128, 8, 128], F32)
    WsT = M.alloc("WsT", [128, 8, 128], BF16)
    ones_b = M.alloc("ones_b", [128, 128], BF16)
    tri_b = M.alloc("tri_b", [128, 128], BF16)
    strict_b = M.alloc("strict_b", [128, 128], BF16)
    ident_b = M.alloc("ident_b", [128, 128], BF16)
    ident_f = M.alloc("ident_f", [128, 128], F32)
    tri_f = M.alloc("tri_f", [128, 128], F32)
    ones_f = M.alloc("ones_f", [128, 128], F32)
    halo = M.alloc("halo", [128, 32, 3], F32)
    small = M.alloc("small", [128, 16, 32], F32)
    xdt_all = M.alloc("xdt_all", [128, 4, 32], F32)
    small2 = [small, None]
    stt = M.alloc("stt", [128, 2, 6], F32)
    mv = M.alloc("mv", [128, 4], F32)
    sqb = [M.alloc("sqb%d" % i, [128, 512], BF16) for i in range(2)]
    rstd = M.alloc("rstd", [128, 512], F32)
    lnt = M.alloc("lnt", [128, 512], F32)
    h = M.alloc("h", [128, 8, 512], F32)
    n = M.alloc("n", [128, 8, 512], BF16)
    ring = [M.alloc("ring%d" % i, [128, UNIT_ELEMS], BF16) for i in range(RING)]
    big0 = M.used()
    act = M.alloc("act", [128, 32, 512], BF16)
    zs = M.at("zs", [128, 4, 2048], BF16, big0)
    xs_tok = M.at("xs_tok", [128, 4, 2048], BF16, big0 + 16384)
    u0 = M.used()
    ugl = M.alloc("ugl", [128, 8, 512], BF16)
    v_ln = M.alloc("v_ln", [128, 4, 1024], BF16)
    ygT = M.at("ygT", [128, 16, 512], BF16, u0)
    BT = M.alloc("BT", [128, 8, 512], BF16)
    CT = M.alloc("CT", [128, 8, 512], BF16)
    B_tok = M.alloc("B_tok", [128, 4, 1024], BF16)
    merged = M.alloc("merged", [128, 8, 512], BF16)
    Sst = M.alloc("Sst", [128, 2048], F32)
    S_bf = M.alloc("S_bf", [128, 2048], BF16)
    Rq = [M.alloc("Rq%d" % i, [128, 8, 128], BF16) for i in range(2)]
    EMq = [M.alloc("EMq%d" % i, [128, 8, 128], BF16) for i in range(2)]
    cbm = M.alloc("cbm", [128, 8, 128], BF16)
    ftmp = M.alloc("ftmp", [128, 4, 128], F32)
    small2[1] = M.at("small_b", [128, 16, 32], F32, ftmp.off - M.base)
    hct = M.at("hct", [128, 32, 3], F32, ftmp.off - M.base + 13 * 32 * 4)
    t0 = M.used()
    t1 = M.alloc("t1", [128, 2048], F32)
    t2 = M.alloc("t2", [128, 2048], F32)
    yg = M.alloc("yg", [128, 2048], BF16)
    xd = M.alloc("xd", [128, 2048], BF16)
    xin = [M.at("xin%d" % i, [128, 1024], F32, t0 + 4096 * i) for i in range(2)]
    ost = [M.at("ost%d" % i, [128, 1024], F32, t0 + 8192 + 4096 * i) for i in range(2)]
    sg = [M.at("sg%d" % i, [128, 512], F32, t0 + 16384 + 2048 * i) for i in range(2)]
    acc2 = [M.at("acc2_%d" % i, [128, 2, 512], F32, t0 + 4096 * i) for i in range(3)]
    xo2 = [M.at("xo2_%d" % i, [128, 2, 512], BF16, t0 + 12288 + 2048 * i) for i in range(2)]
    hcb = M.at("hcb", [128, 32, 3], F32, small.off - M.base + 13 * 32 * 4)
    vg = [M.at("vg%d" % i, [128, 1024], F32, t0 + 12288 + 4096 * i) for i in range(2)]
    sgt = [M.at("sgt%d" % i, [128, 512], BF16, t0 + 20480 + 1024 * i) for i in range(2)]
    xdt_t = M.alloc("xdt_t", [128, 2048], BF16)
    sbuf_used = M.used()

    psn = [0]

    def PSX(bank, nb=1, dtype=F32, shape=None):
        return M.psum(bank, nb, dtype, shape)

    def PSB(nb=1, dtype=F32, shape=None):
        b = psn[0]
        if b % nb:
            b += nb - b % nb
        if b + nb > 7:
            b = 0
        psn[0] = (b + nb) % 7
        return M.psum(b, nb, dtype, shape)

    cnt = {"evac": 0}

    def alt():
        cnt["evac"] += 1
        return "act" if cnt["evac"] % 2 else "dve"

    def copy_op(eng, dst, src):
        if eng == "act":
            ops("act", lambda e: e.activation(out=dst.ap, in_=src.ap, func=AF.Copy), reads=[src], writes=[dst])
        else:
            ops(eng, lambda e: e.tensor_copy(dst.ap, src.ap), reads=[src], writes=[dst])

    def mm(out, lhsT, rhs, start, stop):
        ops("pe", lambda e: e.matmul(out.ap, lhsT=lhsT.ap, rhs=rhs.ap, start=start, stop=stop), reads=[lhsT, rhs], writes=[out])

    def transpose(out, in_, ident):
        ops("pe", lambda e: e.transpose(out.ap, in_.ap, ident.ap), reads=[in_, ident], writes=[out])

    st = {"next_load": 0, "done": -1, "tile": 0, "gu": 0}

    def unit_src(u):
        name, kcn, c0, ncols = units[u]
        return W[name][:, c0:c0 + ncols].rearrange("(kc p) n -> p kc n", p=128), kcn, ncols

    def emit_load(gu):
        tile, u = divmod(gu, NU)
        slot = gu % RING
        src, kcn, ncols = unit_src(u)
        ne = kcn * ncols
        dst = ring[slot][:, 0:ne]
        if tile == 0:
            nparts = max(1, kcn // 8)
            kp = kcn // nparts
            for pi in range(nparts):
                dpart = ring[slot][:, pi * kp * ncols:(pi + 1) * kp * ncols]
                spart = src[:, pi * kp:(pi + 1) * kp, :]
                ops("pool", lambda e, dpart=dpart, spart=spart: e.dma_start(out=dpart.ap.rearrange("p (k n) -> p k n", k=kp), in_=spart),
                    writes=[dpart], dma="wlp%d_%d" % (slot, pi))
            sc = wscr_b[u].acc(wscr[u, :, 0:ne])
            ops("sp", lambda e: e.dma_start(out=sc.ap, in_=dst.ap), reads=[dst], writes=[sc], dma="ws%d" % slot)
        else:
            sc = wscr_b[u].acc(wscr[u, :, 0:ne])
            ops("sp", lambda e: e.dma_start(out=dst.ap, in_=sc.ap), reads=[sc], writes=[dst], dma="wl%d" % slot)

    def pump():
        total = ntiles * NU
        while st["next_load"] < total and st["next_load"] <= st["done"] + RING:
            emit_load(st["next_load"])
            st["next_load"] += 1

    class UnitView:
        def __init__(self, gu):
            name, kcn, c0, ncols = units[gu % NU]
            self.t = ring[gu % RING]
            self.kcn, self.ncols = kcn, ncols

        def sl(self, kc, a, b):
            o = kc * self.ncols
            return self.t[:, o + a:o + b]

    def take(k):
        g0 = st["gu"]
        while st["next_load"] < g0 + k:
            assert st["next_load"] <= st["done"] + RING, "ring too small for group"
            emit_load(st["next_load"])
            st["next_load"] += 1
        st["gu"] = g0 + k
        return [UnitView(g0 + i) for i in range(k)]

    def release():
        st["done"] = st["gu"] - 1
        pump()

    ops("sp", lambda e: e.dma_start(out=cpp[:].ap, in_=cpp_d), writes=[cpp[:]], dma="c0")
    ops("sp", lambda e: e.dma_start(out=bcs[:].ap, in_=bcs_d[:, 0:96]), writes=[bcs[:]], dma="c1")
    ops("sp", lambda e: e.dma_start(out=t1[:, 0:1024].ap, in_=bcs_d[:, 96:96 + 1024]), writes=[t1[:, 0:1024]], dma="c2")
    ops("sp", lambda e: e.dma_start(out=t2[:, 0:1024].ap, in_=wst_d.rearrange("p g t -> p (g t)")), writes=[t2[:, 0:1024]], dma="c3")
    ops("pool", lambda e: e.memset(ones_f[:].ap, 1.0), writes=[ones_f[:]])
    ops("pool", lambda e: e.memset(ones_b[:].ap, 1.0), writes=[ones_b[:]])

    def mask(dst, pat, cm, cmp):
        ops("pool", lambda e: e.affine_select(out=dst[:].ap, in_=ones_f[:].ap, pattern=[[pat, 128]], compare_op=cmp, fill=0.0, base=0,
                                              channel_multiplier=cm), reads=[ones_f[:]], writes=[dst[:]])
    mask(tri_b, 1, -1, ALU.is_ge)
    mask(tri_f, 1, -1, ALU.is_ge)
    mask(strict_b, -1, 1, ALU.is_gt)
    mask(ident_b, 1, -1, ALU.is_equal)
    mask(ident_f, 1, -1, ALU.is_equal)
    ops("act", lambda e: e.activation(out=bcs[:, B_ALOG:B_ALOG + 32].ap, in_=bcs[:, B_ALOG:B_ALOG + 32].ap, func=AF.Exp),
        reads=[bcs[:, B_ALOG:B_ALOG + 32]], writes=[bcs[:, B_ALOG:B_ALOG + 32]])
    ops("dve", lambda e: e.tensor_scalar(out=bcs[:, B_ALOG:B_ALOG + 32].ap, in0=bcs[:, B_ALOG:B_ALOG + 32].ap, scalar1=-1.0, scalar2=None, op0=ALU.mult),
        reads=[bcs[:, B_ALOG:B_ALOG + 32]], writes=[bcs[:, B_ALOG:B_ALOG + 32]])
    t2v = t2[:, 0:1024]
    ops("dve", lambda e: e.tensor_copy(WsT[:].ap, t2v.ap.rearrange("p (g t) -> p g t", g=8)), reads=[t2v], writes=[WsT[:]])
    ops("dve", lambda e: e.memset(WsT[64:128, :, 0:64].ap, 0.0), writes=[WsT[:]])
    pw = PSB(2, F32, [8, 128])
    for hh in range(2):
        mm(pw[:, 4 * hh:4 * hh + 4, :], ones_b[:], WsT[:, 4 * hh:4 * hh + 4, :], True, True)
    for g in range(8):
        ops("dve", lambda e, g=g: e.scalar_tensor_tensor(out=Kc[:, g, :].ap, in0=pw[:, g, :].ap, scalar=cpp[:, P_LNB + g:P_LNB + g + 1].ap,
                                                         in1=t1[:, 128 * g:128 * g + 128].ap, op0=ALU.mult, op1=ALU.add),
            reads=[pw[:, g, :], cpp[:], t1[:, 128 * g:128 * g + 128]], writes=[Kc[:, g, :]])

    pump()
    try:
        chk(0)
    except _Stop:
        DEBUG_STOP[0] = -1

    NORM_BANK = 7
    nst = {"cnt": 0}

    pend = []

    def norm_acc(kc, cols=None, defer=True):
        ps = PSX(NORM_BANK)
        c0, c1 = cols if cols is not None else (0, TT)
        sq = sqb[kc % 2]
        norm_flush()
        ops("act", lambda e: e.activation(out=sq[:, c0:c1].ap, in_=h[:, kc, c0:c1].ap, func=AF.Square), reads=[h[:, kc, c0:c1]], writes=[sq[:, c0:c1]])
        pend.append(lambda: mm(ps[:, c0:c1], ones_b[:], sq[:, c0:c1], kc == 0, kc == KC - 1))
        if not defer:
            norm_flush()

    def norm_flush():
        while pend:
            pend.pop(0)()

    def norm_finish(col, dst_is_h=False):
        norm_flush()
        ps = PSX(NORM_BANK)
        ops("act", lambda e: e.activation(out=lnt[:].ap, in_=ps[:].ap, func=AF.Ln, scale=1.0 / D, bias=EPS), reads=[ps[:]], writes=[lnt[:]])
        ops("act", lambda e: e.activation(out=rstd[:].ap, in_=lnt[:].ap, func=AF.Exp, scale=-0.5), reads=[lnt[:]], writes=[rstd[:]])
        for kc in range(KC):
            dst = h[:, kc, :] if dst_is_h else n[:, kc, :]
            ops("dve", lambda e, kc=kc, dst=dst: e.scalar_tensor_tensor(out=dst.ap, in0=h[:, kc, :].ap, scalar=cpp[:, col + kc:col + kc + 1].ap, in1=rstd[:].ap,
                                                                        op0=ALU.mult, op1=ALU.mult),
                reads=[h[:, kc, :], cpp[:], rstd[:]], writes=[dst])

    def ffn(col):
        norm_finish(col)
        chk(2)
        for i in range(8):
            Ug, Uu = take(2)
            for jj in range(4):
                j = 4 * i + jj
                pg = PSB()
                for kc in range(KC):
                    mm(pg[:], Ug.sl(kc, 128 * jj, 128 * jj + 128), n[:, kc, :], kc == 0, kc == KC - 1)
                pu = PSB()
                for kc in range(KC):
                    mm(pu[:], Uu.sl(kc, 128 * jj, 128 * jj + 128), n[:, kc, :], kc == 0, kc == KC - 1)
                s_ = sg[j % 2]
                ops("act", lambda e, pg=pg, s_=s_: e.activation(out=s_[:].ap, in_=pg[:].ap, func=AF.Silu), reads=[pg[:]], writes=[s_[:]])
                ops("dve", lambda e, pu=pu, s_=s_, j=j: e.tensor_tensor(out=act[:, j, :].ap, in0=pu[:].ap, in1=s_[:].ap, op=ALU.mult),
                    reads=[pu[:], s_[:]], writes=[act[:, j, :]])
            release()
        for m in range(8):
            (U,) = take(1)
            po = PSB()
            for kc in range(32):
                mm(po[:], U.sl(kc, 0, 128), act[:, kc, :], kc == 0, kc == 31)
            ops("dve", lambda e, po=po, m=m: e.scalar_tensor_tensor(out=h[:, m, :].ap, in0=po[:].ap, scalar=0.5, in1=h[:, m, :].ap, op0=ALU.mult, op1=ALU.add),
                reads=[po[:], h[:, m, :]], writes=[h[:, m, :]])
            norm_acc(m)
            release()

    prefetched = set()

    def load_dma(tile, b):
        tok0 = tile * TT
        xi = xin[b % 2]
        ops("sp", lambda e: e.dma_start(out=xi[:].ap, in_=x_d[tok0 + 128 * b:tok0 + 128 * b + 128, :]), writes=[xi[:]], dma="xi%d" % (b % 2))

    def prefetch_x(tile):
        for b in range(2):
            load_dma(tile, b)
            prefetched.add((tile, b))

    def load_blk(tile, b):
        xi = xin[b % 2]
        if (tile, b) not in prefetched:
            load_dma(tile, b)
        for hh in range(2):
            ps = PSB(1, F32, [4, 128])
            for q in range(4):
                kc = 4 * hh + q
                transpose(ps[:, q, :], xi[:, 128 * kc:128 * kc + 128], ident_f[:])
            copy_op(alt(), h[:, 4 * hh:4 * hh + 4, 128 * b:128 * b + 128], ps[:])
        for kc in range(KC):
            norm_acc(kc, (128 * b, 128 * b + 128))

    def store_blk(tile, b):
        tok0 = tile * TT
        o = ost[b % 2]
        for hh in range(2):
            ps = PSB()
            for q in range(4):
                kc = 4 * hh + q
                transpose(ps[:, 128 * q:128 * q + 128], h[:, kc, 128 * b:128 * b + 128], ident_f[:])
            copy_op(alt(), o[:, 512 * hh:512 * hh + 512], ps[:])
        od = out_b[b % 2].acc(out_d[tok0 + 128 * b:tok0 + 128 * b + 128, :])
        ops("sp", lambda e: e.dma_start(out=od.ap, in_=o[:].ap), reads=[o[:]], writes=[od], dma="oo%d" % (b % 2))

    def boundary(tile_done, tile_next):
        if tile_done is not None:
            norm_finish(P_FIN, dst_is_h=True)
        for b in range(NBLK):
            if tile_done is not None:
                store_blk(tile_done, b)
            if tile_next is not None:
                load_blk(tile_next, b)

    def bc3(acc_, shape, axis):
        return acc_.ap.unsqueeze(axis).to_broadcast(shape)

    def mixer(first_of_seq):
        if first_of_seq:
            ops("pool", lambda e: e.memset(Sst[:].ap, 0.0), writes=[Sst[:]])
            ops("pool", lambda e: e.memset(S_bf[:].ap, 0.0), writes=[S_bf[:]])
            ops("pool", lambda e: e.memset(halo[:].ap, 0.0), writes=[halo[:]])
        norm_finish(P_MIX)
        Us = take(2)
        for m in range(8):
            U = Us[m // 4]
            ps = PSB()
            for kc in range(KC):
                mm(ps[:], U.sl(kc, 128 * (m % 4), 128 * (m % 4) + 128), n[:, kc, :], kc == 0, kc == KC - 1)
            ops("act", lambda e, ps=ps, m=m: e.activation(out=ugl[:, m, :].ap, in_=ps[:].ap, func=AF.Gelu), reads=[ps[:]], writes=[ugl[:, m, :]])
        release()
        chk(4)
        Vs = take(2)
        for b in range(NBLK):
            vgb = vg[b % 2]
            for hh in range(2):
                ps = PSB()
                for kc in range(KC):
                    mm(ps[:], n[:, kc, 128 * b:128 * b + 128], Vs[hh].sl(kc, 0, 512), kc == 0, kc == KC - 1)
                ops("act", lambda e, ps=ps, vgb=vgb, hh=hh: e.activation(out=vgb[:, 512 * hh:512 * hh + 512].ap, in_=ps[:].ap, func=AF.Gelu),
                    reads=[ps[:]], writes=[vgb[:, 512 * hh:512 * hh + 512]])
                ops("dve", lambda e, vgb=vgb, hh=hh: e.bn_stats(stt[:, hh, :].ap, vgb[:, 512 * hh:512 * hh + 512].ap),
                    reads=[vgb[:, 512 * hh:512 * hh + 512]], writes=[stt[:, hh, :]])
            ops("dve", lambda e: e.bn_aggr(mv[:, 0:2].ap, stt[:].ap), reads=[stt[:]], writes=[mv[:, 0:2]])
            ops("act", lambda e: e.activation(out=mv[:, 2:3].ap, in_=mv[:, 1:2].ap, func=AF.Ln, bias=EPS), reads=[mv[:, 1:2]], writes=[mv[:, 2:3]])
            ops("act", lambda e: e.activation(out=mv[:, 3:4].ap, in_=mv[:, 2:3].ap, func=AF.Exp, scale=-0.5), reads=[mv[:, 2:3]], writes=[mv[:, 3:4]])
            ops("dve", lambda e, vgb=vgb, b=b: e.tensor_scalar(out=v_ln[:, b, :].ap, in0=vgb[:].ap, scalar1=mv[:, 0:1].ap, scalar2=mv[:, 3:4].ap,
                                                                op0=ALU.subtract, op1=ALU.mult),
                reads=[vgb[:], mv[:]], writes=[v_ln[:, b, :]])
        release()
        chk(7)
        for q in range(4):
            (Z,) = take(1)
            for b in range(NBLK):
                ps = PSB()
                for kc in range(KC):
                    mm(ps[:], n[:, kc, 128 * b:128 * b + 128], Z.sl(kc, 0, 512), kc == 0, kc == KC - 1)
                ops("act", lambda e, ps=ps, b=b, q=q: e.activation(out=zs[:, b, 512 * q:512 * q + 512].ap, in_=ps[:].ap, func=AF.Silu),
                    reads=[ps[:]], writes=[zs[:, b, 512 * q:512 * q + 512]])
            release()
        chk(5)
        for b in range(NBLK):
            for hh in range(2):
                ps = PSB(1, F32, [4, 128])
                for gg in range(4):
                    g = 4 * hh + gg
                    mm(ps[:, gg, :], v_ln[:, b, 128 * g:128 * g + 128], WsT[:, g, :], True, True)
                for gg in range(4):
                    g = 4 * hh + gg
                    ops("dve", lambda e, ps=ps, gg=gg, g=g: e.scalar_tensor_tensor(out=ftmp[:, gg, :].ap, in0=ps[:, gg, :].ap,
                                                                                    scalar=cpp[:, P_LNG + g:P_LNG + g + 1].ap, in1=Kc[:, g, :].ap,
                                                                                    op0=ALU.mult, op1=ALU.add),
                        reads=[ps[:, gg, :], cpp[:], Kc[:, g, :]], writes=[ftmp[:, gg, :]])
                dst = ugl[:, 4 * hh:4 * hh + 4, 128 * b:128 * b + 128]
                ops("pool", lambda e, dst=dst: e.tensor_tensor(out=dst.ap, in0=dst.ap, in1=ftmp[:].ap, op=ALU.mult), reads=[dst, ftmp[:]], writes=[dst])
        chk(6)
        for hh in range(2):
            GA, WA = take(2)
            for mm_ in range(4):
                m = 4 * hh + mm_
                p1 = PSB()
                for kc in range(KC):
                    mm(p1[:], GA.sl(kc, 128 * mm_, 128 * mm_ + 128), n[:, kc, :], kc == 0, kc == KC - 1)
                p2 = PSB()
                for kc in range(KC):
                    mm(p2[:], WA.sl(kc, 128 * mm_, 128 * mm_ + 128), ugl[:, kc, :], kc == 0, kc == KC - 1)
                s_ = sgt[m % 2]
                ops("act", lambda e, p1=p1, s_=s_: e.activation(out=s_[:].ap, in_=p1[:].ap, func=AF.Sigmoid), reads=[p1[:]], writes=[s_[:]])
                ops("dve", lambda e, p2=p2, s_=s_, m=m: e.tensor_tensor(out=merged[:, m, :].ap, in0=p2[:].ap, in1=s_[:].ap, op=ALU.mult),
                    reads=[p2[:], s_[:]], writes=[merged[:, m, :]])
            release()
        chk(8)
        xunit = {}

        def xbc_A(sidx):
            i, sgp = divmod(sidx, 2)
            if sgp == 0:
                (xunit["X"],) = take(1)
            X = xunit["X"]
            cc0 = 2 * sidx
            ps2 = PSX(2 * (sidx % 3), 2, F32, [2, 512])
            for c2 in range(2):
                jj = 2 * sgp + c2
                for kc in range(KC):
                    mm(ps2[:, c2, :], X.sl(kc, 128 * jj, 128 * jj + 128), n[:, kc, :], kc == 0, kc == KC - 1)
            if sgp == 1:
                release()
            ac2 = acc2[sidx % 3]
            for c2 in range(2):
                cc = cc0 + c2
                ops("act", lambda e, c2=c2, cc=cc: e.activation(out=ac2[:, c2, :].ap, in_=ps2[:, c2, :].ap, func=AF.Identity,
                                                               scale=cpp[:, P_CW + 4 * cc + 3:P_CW + 4 * cc + 4].ap,
                                                               bias=cpp[:, P_CB + cc:P_CB + cc + 1].ap),
                    reads=[ps2[:, c2, :], cpp[:]], writes=[ac2[:, c2, :]])
            return ps2, ac2

        def xbc_B(sidx, ps2, ac2):
            cc0 = 2 * sidx
            for k in (2, 1, 0):
                sft = 3 - k
                for c2 in range(2):
                    cc = cc0 + c2
                    ops("dve", lambda e, c2=c2, cc=cc, k=k, sft=sft: e.scalar_tensor_tensor(
                        out=ac2[:, c2, sft:512].ap, in0=ps2[:, c2, 0:512 - sft].ap, scalar=cpp[:, P_CW + 4 * cc + k:P_CW + 4 * cc + k + 1].ap,
                        in1=ac2[:, c2, sft:512].ap, op0=ALU.mult, op1=ALU.add),
                        reads=[ps2[:, c2, :], cpp[:], ac2[:, c2, :]], writes=[ac2[:, c2, :]])
            ops("dve", lambda e: e.tensor_tensor(out=ac2[:, :, 0:3].ap, in0=ac2[:, :, 0:3].ap, in1=hcb[:, cc0:cc0 + 2, :].ap, op=ALU.add),
                reads=[ac2[:, 0, 0:4], ac2[:, 1, 0:4], hcb[:]], writes=[ac2[:, 0, 0:4], ac2[:, 1, 0:4]])
            ops("dve", lambda e: e.tensor_copy(halo[:, cc0:cc0 + 2, :].ap, ps2[:, :, 509:512].ap), reads=[ps2[:]], writes=[halo[:, cc0:cc0 + 2, :]])

        def xbc_C(sidx, ac2):
            cc0 = 2 * sidx
            xo_ = xo2[sidx % 2]
            for c2 in range(2):
                cc = cc0 + c2
                if cc < 16:
                    dst = xo_[:, c2, :]
                elif cc < 24:
                    dst = BT[:, cc - 16, :]
                else:
                    dst = CT[:, cc - 24, :]
                ops("act", lambda e, c2=c2, dst=dst: e.activation(out=dst.ap, in_=ac2[:, c2, :].ap, func=AF.Silu), reads=[ac2[:, c2, :]], writes=[dst])
            if cc0 < 24:
                pt = PSX(6 + sidx % 2, 1, BF16, [4, 2, 128])
                for c2 in range(2):
                    cc = cc0 + c2
                    for b in range(NBLK):
                        srcb = (xo_[:, c2, 128 * b:128 * b + 128] if cc < 16 else BT[:, cc - 16, 128 * b:128 * b + 128])
                        transpose(pt[:, b, c2, :], srcb, ident_b[:])
                if cc0 < 16:
                    dstt = xs_tok[:, :, 128 * cc0:128 * cc0 + 256]
                else:
                    dstt = B_tok[:, :, 128 * (cc0 - 16):128 * (cc0 - 16) + 256]
                ptf = PSX(6 + sidx % 2, 1, BF16, [4, 256])
                copy_op(alt(), dstt, ptf[:])

        cw3 = cpp[:, P_CW:P_CW + 128]
        for k in (0, 1, 2):
            w_ = 3 - k
            dstk = hcb[:, :, 0:w_] if k == 0 else hct[:, :, 0:w_]
            ops("dve", lambda e, k=k, w_=w_, dstk=dstk: e.tensor_tensor(
                out=dstk.ap, in0=halo[:, :, k:3].ap,
                in1=cw3.ap.rearrange("p (j q) -> p j q", q=4)[:, :, k:k + 1].to_broadcast([128, 32, w_]), op=ALU.mult),
                reads=[halo[:], cpp[:]], writes=[dstk])
            if k > 0:
                ops("dve", lambda e, w_=w_: e.tensor_tensor(out=hcb[:, :, 0:w_].ap, in0=hcb[:, :, 0:w_].ap, in1=hct[:, :, 0:w_].ap, op=ALU.add),
                    reads=[hcb[:], hct[:]], writes=[hcb[:]])
        stA = {0: xbc_A(0)}
        for sidx in range(16):
            if sidx + 1 < 16:
                stA[sidx + 1] = xbc_A(sidx + 1)
            ps2_, ac2_ = stA.pop(sidx)
            xbc_B(sidx, ps2_, ac2_)
            xbc_C(sidx, ac2_)
        chk(9)
        (DT,) = take(1)
        pd = PSB(1, F32, [16, 32])
        for b in range(NBLK):
            for kc in range(KC):
                mm(pd[:, b, :], n[:, kc, 128 * b:128 * b + 128], DT.sl(kc, 0, 32), kc == 0, kc == KC - 1)
        ops("dve", lambda e: e.tensor_tensor(out=xdt_all[:].ap, in0=pd[:, 0:4, :].ap, in1=bc3(bcs[:, B_DTB:B_DTB + 32], [128, 4, 32], 1), op=ALU.add),
            reads=[pd[:, 0:4, :], bcs[:]], writes=[xdt_all[:]])
        release()
        chk(10)
        ssd_tile()
        chk(12)
        for hh in range(2):
            GB, WB0, WB1 = take(3)
            for mm_ in range(4):
                m = 4 * hh + mm_
                WB = (WB0, WB1)[mm_ // 2]
                p1 = PSB()
                for kc in range(KC):
                    mm(p1[:], GB.sl(kc, 128 * mm_, 128 * mm_ + 128), n[:, kc, :], kc == 0, kc == KC - 1)
                p2 = PSB()
                for kc in range(16):
                    mm(p2[:], WB.sl(kc, 128 * (mm_ % 2), 128 * (mm_ % 2) + 128), ygT[:, kc, :], kc == 0, kc == 15)
                s_ = sgt[m % 2]
                g_ = sg[m % 2]
                ops("act", lambda e, p1=p1, s_=s_: e.activation(out=s_[:].ap, in_=p1[:].ap, func=AF.Sigmoid), reads=[p1[:]], writes=[s_[:]])
                ops("dve", lambda e, p2=p2, s_=s_, g_=g_: e.tensor_tensor(out=g_[:].ap, in0=p2[:].ap, in1=s_[:].ap, op=ALU.mult),
                    reads=[p2[:], s_[:]], writes=[g_[:]])
                ops("pool", lambda e, g_=g_, m=m: e.tensor_tensor(out=merged[:, m, :].ap, in0=merged[:, m, :].ap, in1=g_[:].ap, op=ALU.add),
                    reads=[merged[:, m, :], g_[:]], writes=[merged[:, m, :]])
            release()
        chk(13)
        WOs = take(2)
        for m in range(8):
            WO = WOs[m // 4]
            ps = PSB()
            for kc in range(KC):
                mm(ps[:], WO.sl(kc, 128 * (m % 4), 128 * (m % 4) + 128), merged[:, kc, :], kc == 0, kc == KC - 1)
            ops("dve", lambda e, ps=ps, m=m: e.tensor_tensor(out=h[:, m, :].ap, in0=ps[:].ap, in1=h[:, m, :].ap, op=ALU.add),
                reads=[ps[:], h[:, m, :]], writes=[h[:, m, :]])
            norm_acc(m)
        release()

    def ssd_tile():
        def smb(b):
            sm_t = small2[b % 2]
            return lambda i, w=32: sm_t[:, i, 0:w]

        def H1(b):
            sm = smb(b)
            sm_t = small2[b % 2]
            blk = slice(128 * b, 128 * b + 128)
            xdt_b = xdt_all[:, b, :]
            ops("dve", lambda e: e.scalar_tensor_tensor(out=sm(0).ap, in0=xdt_b.ap, scalar=-1.0, in1=xdt_b.ap, op0=ALU.mult, op1=ALU.max), reads=[xdt_b], writes=[sm(0)])
            ops("act", lambda e: e.activation(out=sm(1).ap, in_=sm(0).ap, func=AF.Exp, scale=-1.0), reads=[sm(0)], writes=[sm(1)])
            ops("act", lambda e: e.activation(out=sm(2).ap, in_=sm(1).ap, func=AF.Ln, bias=1.0), reads=[sm(1)], writes=[sm(2)])
            ops("dve", lambda e: e.scalar_tensor_tensor(out=sm(3).ap, in0=xdt_b.ap, scalar=0.0, in1=sm(2).ap, op0=ALU.max, op1=ALU.add),
                reads=[xdt_b, sm(2)], writes=[sm(3)])
            ops("dve", lambda e: e.tensor_tensor(out=sm(4).ap, in0=sm(3).ap, in1=bcs[:, B_ALOG:B_ALOG + 32].ap, op=ALU.mult), reads=[sm(3), bcs[:]], writes=[sm(4)])
            pc = PSX(4, 1, F32, [16, 32])
            mm(pc[:, 0, :], tri_f[:], sm(4), True, True)
            mm(pc[:, 1, :], ones_f[:], sm(4), True, True)
            ops("act", lambda e: e.activation(out=sm_t[:, 5:7, :].ap, in_=pc[:, 0:2, :].ap, func=AF.Exp), reads=[pc[:, 0:2, :]], writes=[sm_t[:, 5:7, :]])
            ops("dve", lambda e: e.tensor_copy(sm(10).ap, pc[:, 0, :].ap), reads=[pc[:, 0:2, :]], writes=[sm(10)])
            ops("dve", lambda e: e.tensor_tensor(out=sm(9).ap, in0=pc[:, 1, :].ap, in1=sm(10).ap, op=ALU.subtract), reads=[pc[:, 0:2, :], sm(10)], writes=[sm(9)])
            ops("act", lambda e: e.activation(out=sm(7).ap, in_=sm(9).ap, func=AF.Exp), reads=[sm(9)], writes=[sm(7)])
            xs_b = xs_tok[:, b, :]
            ops("dve", lambda e: e.tensor_tensor(out=xdt_t[:].ap.rearrange("p (k j) -> p k j", k=32), in0=xs_b.ap.rearrange("p (k j) -> p k j", k=32),
                                                 in1=bc3(sm(3), [128, 32, 64], 2), op=ALU.mult), reads=[xs_b, sm(3)], writes=[xdt_t[:]])
            ops("dve", lambda e: e.tensor_tensor(out=xd[:].ap.rearrange("p (k j) -> p k j", k=32), in0=xdt_t[:].ap.rearrange("p (k j) -> p k j", k=32),
                                                  in1=bc3(sm(7), [128, 32, 64], 2), op=ALU.mult), reads=[xdt_t[:], sm(7)], writes=[xd[:]])
            pcb = PSX(6, 2, F32, [8, 128])
            for g in range(8):
                mm(pcb[:, g, :], BT[:, g, blk], CT[:, g, blk], True, True)
            ops("dve", lambda e: e.tensor_tensor(out=cbm[:].ap, in0=pcb[:].ap, in1=bc3(tri_b[:], [128, 8, 128], 1), op=ALU.mult),
                reads=[pcb[:], tri_b[:]], writes=[cbm[:]])

        def H2(b):
            sm_t = small2[b % 2]
            blk = slice(128 * b, 128 * b + 128)
            pyo_f = PSX(4, 4, F32)
            for g in range(8):
                mm(pyo_f[:, 256 * g:256 * g + 256], CT[:, g, blk], S_bf[:, 256 * g:256 * g + 256], True, True)
            for hf in range(2):
                c0, c1 = 1024 * hf, 1024 * hf + 1024
                pyo_h = PSX(4 + 2 * hf, 2, F32, [16, 64])
                ops("dve", lambda e, c0=c0, c1=c1, pyo_h=pyo_h, hf=hf: e.tensor_tensor(out=t1[:, c0:c1].ap.rearrange("p (k j) -> p k j", k=16), in0=pyo_h[:].ap,
                                                                                 in1=bc3(sm_t[:, 5, 16 * hf:16 * hf + 16], [128, 16, 64], 2), op=ALU.mult),
                    reads=[pyo_h[:], sm_t[:, 5, 0:32]], writes=[t1[:, c0:c1]])
                ops("dve", lambda e, c0=c0, c1=c1, hf=hf: e.tensor_tensor(out=t2[:, c0:c1].ap.rearrange("p (k j) -> p k j", k=16),
                                                                       in0=xs_tok[:, b, c0:c1].ap.rearrange("p (k j) -> p k j", k=16),
                                                                       in1=bc3(bcs[:, B_DSK + 16 * hf:B_DSK + 16 * hf + 16], [128, 16, 64], 2), op=ALU.mult),
                    reads=[xs_tok[:, b, c0:c1], bcs[:]], writes=[t2[:, c0:c1]])

        def U(b):
            sm = smb(b)
            pst = PSX(4, 4, F32)
            for g in range(8):
                mm(pst[:, 256 * g:256 * g + 256], B_tok[:, b, 128 * g:128 * g + 128], xd[:, 256 * g:256 * g + 256], True, True)
            ops("dve", lambda e: e.tensor_tensor(out=Sst[:].ap.rearrange("p (k j) -> p k j", k=32), in0=Sst[:].ap.rearrange("p (k j) -> p k j", k=32),
                                                  in1=bc3(sm(6), [128, 32, 64], 2), op=ALU.mult), reads=[Sst[:], sm(6)], writes=[Sst[:]])
            ops("dve", lambda e: e.tensor_tensor(out=Sst[:].ap, in0=pst[:].ap, in1=Sst[:].ap, op=ALU.add), reads=[pst[:], Sst[:]], writes=[Sst[:]])
            ops("act", lambda e: e.activation(out=S_bf[:].ap, in_=Sst[:].ap, func=AF.Copy), reads=[Sst[:]], writes=[S_bf[:]])

        def Q(b):
            sm_t = small2[b % 2]
            pyd = PSX(0, 4, F32)
            st_ = {}

            def R(q):
                def f():
                    R_ = Rq[q % 2]
                    ops("dve", lambda e: e.tensor_tensor(out=R_[:].ap, in0=bc3(tri_b[:], [128, 8, 128], 1),
                                                          in1=bc3(sm_t[:, 4, 8 * q:8 * q + 8], [128, 8, 128], 2), op=ALU.mult),
                        reads=[tri_b[:], sm_t[:, 4, 0:32]], writes=[R_[:]])
                return f

            def seg(q):
                def f():
                    R_ = Rq[q % 2]
                    pseg = PSX(4, 2, F32, [8, 128])
                    st_[q] = pseg
                    for hh in range(2):
                        mm(pseg[:, 4 * hh:4 * hh + 4, :], strict_b[:], R_[:, 4 * hh:4 * hh + 4, :], True, True)
                return f

            def E(q):
                def f():
                    EM_ = EMq[q % 2]
                    pseg = st_[q]
                    ops("act", lambda e: e.activation(out=EM_[:].ap, in_=pseg[:].ap, func=AF.Exp), reads=[pseg[:]], writes=[EM_[:]])
                return f

            def Mq(q):
                def f():
                    EM_ = EMq[q % 2]
                    ops("dve", lambda e: e.tensor_tensor(out=EM_[:].ap.rearrange("p (g k) t -> p g k t", g=2),
                                                         in0=EM_[:].ap.rearrange("p (g k) t -> p g k t", g=2),
                                                         in1=cbm[:, 2 * q:2 * q + 2, :].ap.unsqueeze(2).to_broadcast([128, 2, 4, 128]), op=ALU.mult),
                        reads=[EM_[:], cbm[:, 2 * q:2 * q + 2, :]], writes=[EM_[:]])
                return f

            def yd(q):
                def f():
                    EM_ = EMq[q % 2]
                    for kk in range(8):
                        k = 8 * q + kk
                        mm(pyd[:, 64 * k:64 * k + 64], EM_[:, kk, :], xdt_t[:, 64 * k:64 * k + 64], True, True)
                return f
            return [R(0), R(1), seg(0), E(0), seg(1), Mq(0), E(1), yd(0), R(2), seg(2), Mq(1), E(2), yd(1), R(3), seg(3), Mq(2), E(3), yd(2), Mq(3), yd(3)]

        def T(b):
            sm_t = small2[b % 2]
            blk = slice(128 * b, 128 * b + 128)
            HC = [(0, 1024), (1024, 2048)]
            srow = [8, 12]

            def t0_():
                for hf, (c0, c1) in enumerate(HC):
                    pyd_h = PSX(2 * hf, 2, F32)
                    ops("dve", lambda e, c0=c0, c1=c1, pyd_h=pyd_h: e.tensor_tensor(out=t1[:, c0:c1].ap, in0=pyd_h[:].ap, in1=t1[:, c0:c1].ap, op=ALU.add),
                        reads=[pyd_h[:], t1[:, c0:c1]], writes=[t1[:, c0:c1]])
                    ops("dve", lambda e, c0=c0, c1=c1: e.tensor_tensor(out=t1[:, c0:c1].ap, in0=t1[:, c0:c1].ap, in1=t2[:, c0:c1].ap, op=ALU.add),
                        reads=[t1[:, c0:c1], t2[:, c0:c1]], writes=[t1[:, c0:c1]])

            def t1_():
                for hf, (c0, c1) in enumerate(HC):
                    ops("dve", lambda e, c0=c0, c1=c1: e.tensor_tensor(out=t1[:, c0:c1].ap, in0=t1[:, c0:c1].ap, in1=zs[:, b, c0:c1].ap, op=ALU.mult),
                        reads=[t1[:, c0:c1], zs[:, b, c0:c1]], writes=[t1[:, c0:c1]])

            def t2_():
                for hf, (c0, c1) in enumerate(HC):
                    ops("act", lambda e, c0=c0, c1=c1: e.activation(out=t2[:, c0:c1].ap, in_=t1[:, c0:c1].ap, func=AF.Square), reads=[t1[:, c0:c1]], writes=[t2[:, c0:c1]])

            def t3_():
                for hf, (c0, c1) in enumerate(HC):
                    r = srow[hf]
                    ops("dve", lambda e, c0=c0, c1=c1, r=r: e.tensor_reduce(out=sm_t[:, r, 0:4].ap, in_=t2[:, c0:c1].ap.rearrange("p (g j) -> p g j", g=4), axis=AX.X, op=ALU.add),
                        reads=[t2[:, c0:c1]], writes=[sm_t[:, r, 0:4]])

            def t4_():
                for hf in range(2):
                    r = srow[hf]
                    ops("act", lambda e, r=r: e.activation(out=sm_t[:, r + 1, 0:4].ap, in_=sm_t[:, r, 0:4].ap, func=AF.Ln, scale=1.0 / 256, bias=EPS),
                        reads=[sm_t[:, r, 0:4]], writes=[sm_t[:, r + 1, 0:4]])
                    ops("act", lambda e, r=r: e.activation(out=sm_t[:, r, 0:4].ap, in_=sm_t[:, r + 1, 0:4].ap, func=AF.Exp, scale=-0.5),
                        reads=[sm_t[:, r + 1, 0:4]], writes=[sm_t[:, r, 0:4]])

            def t5_():
                for hf, (c0, c1) in enumerate(HC):
                    r = srow[hf]
                    ops("dve", lambda e, c0=c0, c1=c1, r=r: e.tensor_tensor(out=yg[:, c0:c1].ap.rearrange("p (g j) -> p g j", g=4),
                                                                        in0=t1[:, c0:c1].ap.rearrange("p (g j) -> p g j", g=4),
                                                                        in1=bc3(sm_t[:, r, 0:4], [128, 4, 256], 2), op=ALU.mult),
                        reads=[t1[:, c0:c1], sm_t[:, r, 0:4]], writes=[yg[:, c0:c1]])

            def t6_():
                for hf in range(2):
                    pt = PSX(6 + hf, 1, BF16, [8, 128])
                    for c8 in range(8):
                        cc = 8 * hf + c8
                        transpose(pt[:, c8, :], yg[:, 128 * cc:128 * cc + 128], ident_b[:])
                    dst = ygT[:, 8 * hf:8 * hf + 8, blk]
                    ops("dve", lambda e, pt=pt, dst=dst, hf=hf: e.tensor_tensor(out=dst.ap, in0=pt[:].ap,
                                                                             in1=bc3(cpp[:, P_SSMG + 8 * hf:P_SSMG + 8 * hf + 8], [128, 8, 128], 2), op=ALU.mult),
                        reads=[pt[:], cpp[:]], writes=[dst])
            return [t0_, t1_, t2_, t3_, t4_, t5_, t6_]

        def interleave(A, B):
            A[0]()
            rest = A[1:-1]
            nb, na = len(B), len(rest)
            bi = 0
            for i, ta in enumerate(rest):
                tgt = (i + 1) * nb // (na + 1)
                while bi < tgt:
                    B[bi]()
                    bi += 1
                ta()
            while bi < nb:
                B[bi]()
                bi += 1
            A[-1]()

        H1(0)
        H2(0)
        U(0)
        for f in Q(0):
            f()
        for b in range(NBLK):
            if b + 1 < NBLK:
                H1(b + 1)
                interleave(T(b), Q(b + 1))
                H2(b + 1)
                U(b + 1)
            else:
                for f in T(b):
                    f()

    try:
        boundary(None, 0)
        for tile in range(ntiles):
            if DEBUG_STOP[0] == -1:
                break
            chk(1)
            ffn(P_FFN1)
            chk(3)
            mixer(tile % tiles_per_seq == 0)
            chk(14)
            if tile + 1 < ntiles:
                prefetch_x(tile + 1)
            ffn(P_FFN2)
            chk(15)
            boundary(tile, tile + 1 if tile + 1 < ntiles else None)
    except _Stop:
        pass
    Sched.limit = None
    print("nops", Sched.nops)
    ops("sp", lambda e: e.nop(), reads=[out_b[0].acc(out_d), out_b[1].acc(out_d)])
    with nc.allow_low_precision("bf16 matmul operands, fp32 accumulation"):
        S.emit(nc)
    S.stats["sbuf_used"] = sbuf_used
    return nc, S


def prep_consts(inp):
    f = lambda a: np.ascontiguousarray(np.asarray(a, dtype=np.float32))
    fm = lambda v, nch: f(v).reshape(nch, 128).T
    cpp = np.zeros((128, NCPP), np.float32)
    cpp[:, P_FFN1:P_FFN1 + 8] = fm(inp["ffn1_norm"][0], 8)
    cpp[:, P_MIX:P_MIX + 8] = fm(inp["mix_norm"][0], 8)
    cpp[:, P_FFN2:P_FFN2 + 8] = fm(inp["ffn2_norm"][0], 8)
    cpp[:, P_FIN:P_FIN + 8] = fm(inp["final_norm"], 8)
    cw = f(inp["conv_w"][0])
    cpp[:, P_CW:P_CW + 128] = cw.T.reshape(32, 128, 4).transpose(1, 0, 2).reshape(128, 128)
    cpp[:, P_CB:P_CB + 32] = fm(inp["conv_b"][0], 32)
    cpp[:, P_SSMG:P_SSMG + 16] = fm(inp["ssm_norm"][0], 16)
    cpp[:, P_LNG:P_LNG + 8] = fm(inp["sgu_ln_g"][0], 8)
    cpp[:, P_LNB:P_LNB + 8] = fm(inp["sgu_ln_b"][0], 8)
    bcs = np.zeros((128, NBCS), np.float32)
    bcs[:, B_DTB:B_DTB + 32] = f(inp["dt_bias"][0])[None, :]
    bcs[:, B_ALOG:B_ALOG + 32] = f(inp["a_log"][0])[None, :]
    bcs[:, B_DSK:B_DSK + 32] = f(inp["d_skip"][0])[None, :]
    bcs[:, B_BS:B_BS + 1024] = f(inp["sgu_b_s"][0]).reshape(1, 1024)
    wst = np.ascontiguousarray(f(inp["sgu_w_s"][0]).transpose(2, 0, 1))
    return cpp, bcs, wst


WMAP = {"ffn1_w_in": "ffn1_w_in", "ffn1_w_out": "ffn1_w_out", "w_in": "w_in", "w_a": "w_a", "w_b": "w_b", "w_o": "w_o",
        "ffn2_w_in": "ffn2_w_in", "ffn2_w_out": "ffn2_w_out"}


def make_in_maps(inp, n_cores, n_seq, seq_len):
    cpp, bcs, wst = prep_consts(inp)
    x = np.asarray(inp["x"], dtype=np.float32)
    shared = {k: np.ascontiguousarray(np.asarray(inp[v], dtype=np.float32)[0]) for k, v in WMAP.items()}
    shared.update(cpp=cpp, bcs=bcs, wst=wst)
    maps = []
    for c in range(n_cores):
        xs = np.ascontiguousarray(x[c * n_seq:(c + 1) * n_seq].reshape(n_seq * seq_len, D))
        d = dict(shared)
        d["x"] = xs
        maps.append(d)
    return maps


def kernel(**inputs):
    from concourse.bass_utils import run_bass_kernel_spmd
    x = np.asarray(inputs["x"])
    B, L, _ = x.shape
    n_cores = 8
    n_seq = B // n_cores
    nc, S = build_nc(n_seq, L)
    maps = make_in_maps(inputs, n_cores, n_seq, L)
    res = run_bass_kernel_spmd(nc, maps, core_ids=list(range(n_cores)))
    out = np.stack([r["out"].reshape(n_seq, L, D) for r in res.results], axis=0).reshape(B, L, D)
    return out.astype(np.float32)
```

```python
import numpy as np
import concourse.bass as bass
import concourse.mybir as mybir

F32 = mybir.dt.float32
BF16 = mybir.dt.bfloat16
AF = mybir.ActivationFunctionType
ALU = mybir.AluOpType
AX = mybir.AxisListType

PAGE = 256
_ESZ = {F32: 4, BF16: 2}


class Buf:
    __slots__ = ("last_w", "readers", "excl")

    def __init__(self, excl=False):
        self.last_w = None
        self.readers = []
        self.excl = excl


class Acc:
    __slots__ = ("ap", "bufs")

    def __init__(self, ap, bufs):
        self.ap = ap
        self.bufs = bufs


class Tn:
    def __init__(self, handle, space, off, shape, dtype, pages):
        self.t = handle
        self.space = space
        self.off = off
        self.shape = list(shape)
        self.dtype = dtype
        self.esz = _ESZ[dtype]
        self.pages = pages
        st = []
        s = 1
        for d in reversed(self.shape[1:]):
            st.append(s)
            s *= d
        self.strides = list(reversed(st))

    def _ranges(self, idx):
        dims = self.shape[1:]
        sel = []
        for i, d in enumerate(dims):
            if i < len(idx):
                k = idx[i]
                if isinstance(k, int):
                    sel.append((k, k + 1))
                else:
                    a = 0 if k.start is None else k.start
                    b = d if k.stop is None else k.stop
                    assert k.step in (None, 1)
                    sel.append((a, b))
            else:
                sel.append((0, d))
        n = len(dims)
        runs = [(0, 1)]
        inner = 1
        j = n - 1
        while j >= 0 and sel[j] == (0, dims[j]):
            inner *= dims[j]
            j -= 1
        if j < 0:
            return [(0, inner)]
        a, b = sel[j]
        base_runs = [(a * self.strides[j], (b - a) * inner)]
        for jj in range(j - 1, -1, -1):
            a, b = sel[jj]
            new = []
            for k in range(a, b):
                for (o, l) in base_runs:
                    new.append((o + k * self.strides[jj], l))
            base_runs = new
        return base_runs

    def __getitem__(self, idx):
        if not isinstance(idx, tuple):
            idx = (idx,)
        ap = self.t[idx]
        bufs = []
        seen = set()
        for (o, l) in self._ranges(idx[1:]):
            lo = self.off + o * self.esz
            hi = self.off + (o + l) * self.esz
            for p in range(lo // PAGE, (hi - 1) // PAGE + 1):
                if p not in seen:
                    seen.add(p)
                    b = self.pages.get(p)
                    if b is None:
                        b = self.pages[p] = Buf()
                    bufs.append(b)
        return Acc(ap, bufs)


class Mem:
    def __init__(self, nc):
        self.nc = nc
        self.base = (nc.sbuf_base + 255) // 256 * 256
        self.top = nc.sbuf_top
        self.cur = self.base
        self.sb_pages = {}
        self.ps_pages = {}
        self.n = 0
        self.ps = nc.alloc_psum_tensor("psall", [128, 4096], F32)

    def at(self, name, shape, dtype, off):
        self.n += 1
        size = int(np.prod(shape[1:])) * _ESZ[dtype]
        assert self.base + off + size <= self.top, (name, off, size, self.top - self.base)
        h = self.nc.alloc_sbuf_tensor_at("%s_%d" % (name, self.n), list(shape), dtype, offset=self.base + off)
        return Tn(h, "sb", self.base + off, shape, dtype, self.sb_pages)

    def alloc(self, name, shape, dtype):
        size = int(np.prod(shape[1:])) * _ESZ[dtype]
        off = self.cur - self.base
        self.cur += (size + 255) // 256 * 256
        assert self.cur <= self.top, ("SBUF overflow", name, self.cur - self.base, self.top - self.base)
        return self.at(name, shape, dtype, off)

    def used(self):
        return self.cur - self.base

    def psum(self, bank, nbanks=1, dtype=F32, shape=None):
        ap = self.ps[:, bank * 512:(bank + nbanks) * 512]
        n = nbanks * 512
        if dtype != F32:
            ap = ap.bitcast(dtype)
            n = n * 4 // _ESZ[dtype]
        return PsView(ap, bank, nbanks, dtype, n, self.ps_pages, shape)


class PsView:
    def __init__(self, ap, bank, nbanks, dtype, n, pages, shape):
        self.ap0 = ap
        self.bufs = []
        for b in range(bank, bank + nbanks):
            if b not in pages:
                pages[b] = Buf(excl=True)
            self.bufs.append(pages[b])
        if shape is not None:
            names = " ".join("d%d" % i for i in range(len(shape)))
            kw = {"d%d" % i: s for i, s in enumerate(shape)}
            self.ap0 = ap.rearrange("p (%s) -> p %s" % (names, names), **kw)

    def __getitem__(self, idx):
        return Acc(self.ap0[idx], self.bufs)


class DramBuf:
    def __init__(self):
        self.buf = Buf()

    def acc(self, ap):
        return Acc(ap, [self.buf])


class _Stop(Exception):
    pass


class Ins:
    __slots__ = ("eng", "fn", "deps", "is_dma", "sem_key", "sig", "sig_val")

    def __init__(self, eng, fn, is_dma, sem_key):
        self.eng = eng
        self.fn = fn
        self.deps = ()
        self.is_dma = is_dma
        self.sem_key = sem_key
        self.sig = False
        self.sig_val = 0


class Sched:
    ENGS = ("pe", "act", "dve", "pool", "sp")

    def __init__(self, same_eng_raw=True):
        self.streams = {e: [] for e in self.ENGS}
        self.same_eng_raw = same_eng_raw

    nops = 0
    limit = None

    def op(self, eng, fn, reads=(), writes=(), dma=None, acc=False):
        Sched.nops += 1
        if Sched.limit is not None and Sched.nops > Sched.limit:
            raise _Stop()
        ins = Ins(eng, fn, dma is not None, dma)
        is_dma = dma is not None
        deps = set()
        for a in reads:
            for b in a.bufs:
                w = b.last_w
                if w is not None:
                    deps.add(w)
                if b.excl:
                    for r in b.readers:
                        if r.eng != eng:
                            deps.add(r)
        for a in writes:
            for b in a.bufs:
                w = b.last_w
                if w is not None:
                    if not (eng == "pe" and w.eng == "pe"):
                        deps.add(w)
                for r in b.readers:
                    if r is not ins:
                        deps.add(r)
        deps.discard(ins)
        ins.deps = tuple(deps)
        for a in writes:
            for b in a.bufs:
                b.last_w = ins
                b.readers = []
        for a in reads:
            for b in a.bufs:
                b.readers.append(ins)
        self.streams[eng].append(ins)
        return ins

    def emit(self, nc):
        for e in self.ENGS:
            for ins in self.streams[e]:
                if ins.is_dma:
                    ins.sig = True
                for d in ins.deps:
                    d.sig = True
        cnt = {}
        for e in self.ENGS:
            for ins in self.streams[e]:
                if ins.sig:
                    key = ("dma", ins.sem_key) if ins.is_dma else ("eng", e)
                    ins.sem_key = key
                    cnt[key] = cnt.get(key, 0) + (16 if ins.is_dma else 1)
                    ins.sig_val = cnt[key]
        sems = {}
        for k in sorted(cnt):
            sems[k] = nc.alloc_semaphore(name=("s_%s_%s" % k).replace(".", "_"))
        self.maxcnt = dict(cnt)
        nwaits = [0]
        with nc.Block() as block:
            def run(e):
                def body(eng):
                    waited = {}
                    for ins in self.streams[e]:
                        need = {}
                        for d in ins.deps:
                            k = d.sem_key
                            if waited.get(k, 0) >= d.sig_val:
                                continue
                            if need.get(k, 0) < d.sig_val:
                                need[k] = d.sig_val
                        for k, v in need.items():
                            eng.wait_ge(sems[k], v)
                            waited[k] = v
                            nwaits[0] += 1
                        bi = ins.fn(eng)
                        if ins.sig:
                            bi.then_inc(sems[ins.sem_key], 16 if ins.is_dma else 1)
                return body
            block.tensor(run("pe"))
            block.scalar(run("act"))
            block.vector(run("dve"))
            block.gpsimd(run("pool"))
            block.sync(run("sp"))
        self.stats = dict(waits=nwaits[0], n={e: len(self.streams[e]) for e in self.ENGS}, sems=len(sems),
                          maxcnt=max(cnt.values()) if cnt else 0)


D = 1024
KC = 8
TT = 512
NBLK = 4
FF = 4096
EPS = 1e-6
RING = 4
UNIT_ELEMS = 4096

C_U, C_V, C_Z, C_X, C_DT, C_GA, C_GB = 0, 1024, 2048, 4096, 8192, 8224, 9248

P_FFN1, P_MIX, P_FFN2, P_FIN, P_CW, P_CB, P_SSMG, P_LNG, P_LNB = 0, 8, 16, 24, 32, 160, 192, 208, 216
NCPP = 224
B_DTB, B_ALOG, B_DSK, B_BS = 0, 32, 64, 96
NBCS = 96 + 1024


def unit_list():
    L = []

    def ffn(pfx):
        for i in range(8):
            L.append((pfx + "_w_in", 8, 512 * i, 512))
            L.append((pfx + "_w_in", 8, FF + 512 * i, 512))
        for m in range(8):
            L.append((pfx + "_w_out", 32, 128 * m, 128))
    ffn("ffn1")
    for i in range(2):
        L.append(("w_in", 8, C_U + 512 * i, 512))
    for i in range(2):
        L.append(("w_in", 8, C_V + 512 * i, 512))
    for i in range(4):
        L.append(("w_in", 8, C_Z + 512 * i, 512))
    for i in range(2):
        L.append(("w_in", 8, C_GA + 512 * i, 512))
        L.append(("w_a", 8, 512 * i, 512))
    for i in range(8):
        L.append(("w_in", 8, C_X + 512 * i, 512))
    L.append(("w_in", 8, C_DT, 32))
    for i in range(2):
        L.append(("w_in", 8, C_GB + 512 * i, 512))
        L.append(("w_b", 16, 512 * i, 256))
        L.append(("w_b", 16, 512 * i + 256, 256))
    for i in range(2):
        L.append(("w_o", 8, 512 * i, 512))
    ffn("ffn2")
    return L


W_SHAPES = {"ffn1_w_in": (D, 2 * FF), "ffn1_w_out": (FF, D), "w_in": (D, 10272), "w_a": (D, D), "w_b": (2 * D, D),
            "w_o": (D, D), "ffn2_w_in": (D, 2 * FF), "ffn2_w_out": (FF, D)}


DEBUG_STOP = [None]
DEBUG_BLK = [0]
DEBUG_SUB = [None]


def build_nc(n_seq, seq_len):
    def chk(k):
        if DEBUG_STOP[0] == k:
            raise _Stop()
    ntok = n_seq * seq_len
    tiles_per_seq = seq_len // TT
    ntiles = n_seq * tiles_per_seq
    nc = bass.Bass("TRN2", target_bir_lowering=False)
    x_d = nc.dram_tensor("x", [ntok, D], F32, kind="ExternalInput").ap()
    out_d = nc.dram_tensor("out", [ntok, D], F32, kind="ExternalOutput").ap()
    W = {k: nc.dram_tensor(k, list(s), F32, kind="ExternalInput").ap() for k, s in W_SHAPES.items()}
    cpp_d = nc.dram_tensor("cpp", [128, NCPP], F32, kind="ExternalInput").ap()
    bcs_d = nc.dram_tensor("bcs", [128, NBCS], F32, kind="ExternalInput").ap()
    wst_d = nc.dram_tensor("wst", [128, 8, 128], F32, kind="ExternalInput").ap()
    units = unit_list()
    NU = len(units)
    wscr = nc.dram_tensor("wscr", [NU, 128, UNIT_ELEMS], BF16).ap()
    wscr_b = [DramBuf() for _ in range(NU)]
    out_b = [DramBuf(), DramBuf()]

    M = Mem(nc)
    S = Sched()
    ops = S.op

    cpp = M.alloc("cpp", [128, NCPP], F32)
    bcs = M.alloc("bcs", [128, 96], F32)
    Kc = M.alloc("Kc", [128, 8, 128], F32)
    WsT = M.alloc("WsT", [128, 8, 128], BF16)
    ones_b = M.alloc("ones_b", [128, 128], BF16)
    tri_b = M.alloc("tri_b", [128, 128], BF16)
    strict_b = M.alloc("strict_b", [128, 128], BF16)
    ident_b = M.alloc("ident_b", [128, 128], BF16)
    ident_f = M.alloc("ident_f", [128, 128], F32)
    tri_f = M.alloc("tri_f", [128, 128], F32)
    ones_f = M.alloc("ones_f", [128, 128], F32)
    halo = M.alloc("halo", [128, 32, 3], F32)
    small = M.alloc("small", [128, 16, 32], F32)
    xdt_all = M.alloc("xdt_all", [128, 4, 32], F32)
    small2 = [small, None]
    stt = M.alloc("stt", [128, 2, 6], F32)
    mv = M.alloc("mv", [128, 4], F32)
    sqb = [M.alloc("sqb%d" % i, [128, 512], BF16) for i in range(2)]
    rstd = M.alloc("rstd", [128, 512], F32)
    lnt = M.alloc("lnt", [128, 512], F32)
    h = M.alloc("h", [128, 8, 512], F32)
    n = M.alloc("n", [128, 8, 512], BF16)
    ring = [M.alloc("ring%d" % i, [128, UNIT_ELEMS], BF16) for i in range(RING)]
    big0 = M.used()
    act = M.alloc("act", [128, 32, 512], BF16)
    zs = M.at("zs", [128, 4, 2048], BF16, big0)
    xs_tok = M.at("xs_tok", [128, 4, 2048], BF16, big0 + 16384)
    hout = M.at("hout", [128, 8, 512], F32, big0 + 16384)
    u0 = M.used()
    ugl = M.alloc("ugl", [128, 8, 512], BF16)
    v_ln = M.alloc("v_ln", [128, 4, 1024], BF16)
    ygT = M.at("ygT", [128, 16, 512], BF16, u0)
    BT = M.alloc("BT", [128, 8, 512], BF16)
    CT = M.alloc("CT", [128, 8, 512], BF16)
    B_tok = M.alloc("B_tok", [128, 4, 1024], BF16)
    merged = M.alloc("merged", [128, 8, 512], BF16)
    Sst = M.alloc("Sst", [128, 2048], F32)
    S_bf = M.alloc("S_bf", [128, 2048], BF16)
    Rq = [M.alloc("Rq%d" % i, [128, 8, 128], BF16) for i in range(2)]
    EMq = [M.alloc("EMq%d" % i, [128, 8, 128], BF16) for i in range(2)]
    cbm = M.alloc("cbm", [128, 8, 128], BF16)
    ftmp = M.alloc("ftmp", [128, 4, 128], F32)
    small2[1] = M.at("small_b", [128, 16, 32], F32, ftmp.off - M.base)
    hct = M.at("hct", [128, 32, 3], F32, ftmp.off - M.base + 13 * 32 * 4)
    t0 = M.used()
    t1 = M.alloc("t1", [128, 2048], F32)
    t2 = M.alloc("t2", [128, 2048], F32)
    yg = M.alloc("yg", [128, 2048], BF16)
    xd = M.alloc("xd", [128, 2048], BF16)
    xin = [M.at("xin%d" % i, [128, 1024], F32, t0 + 4096 * i) for i in range(2)]
    ost = [M.at("ost%d" % i, [128, 1024], F32, t0 + 8192 + 4096 * i) for i in range(2)]
    sg = [M.at("sg%d" % i, [128, 512], F32, t0 + 16384 + 2048 * i) for i in range(2)]
    acc2 = [M.at("acc2_%d" % i, [128, 2, 512], F32, t0 + 4096 * i) for i in range(3)]
    xo2 = [M.at("xo2_%d" % i, [128, 2, 512], BF16, t0 + 12288 + 2048 * i) for i in range(2)]
    hcb = M.at("hcb", [128, 32, 3], F32, small.off - M.base + 13 * 32 * 4)
    vg = [M.at("vg%d" % i, [128, 1024], F32, t0 + 12288 + 4096 * i) for i in range(2)]
    sgt = [M.at("sgt%d" % i, [128, 512], BF16, t0 + 20480 + 1024 * i) for i in range(2)]
    xdt_t = M.alloc("xdt_t", [128, 2048], BF16)
    sbuf_used = M.used()

    psn = [0]

    def PSX(bank, nb=1, dtype=F32, shape=None):
        return M.psum(bank, nb, dtype, shape)

    def PSB(nb=1, dtype=F32, shape=None):
        b = psn[0]
        if b % nb:
            b += nb - b % nb
        if b + nb > 7:
            b = 0
        psn[0] = (b + nb) % 7
        return M.psum(b, nb, dtype, shape)

    cnt = {"evac": 0}

    def alt():
        cnt["evac"] += 1
        return "act" if cnt["evac"] % 2 else "dve"

    def copy_op(eng, dst, src):
        if eng == "act":
            ops("act", lambda e: e.activation(out=dst.ap, in_=src.ap, func=AF.Copy), reads=[src], writes=[dst])
        else:
            ops(eng, lambda e: e.tensor_copy(dst.ap, src.ap), reads=[src], writes=[dst])

    def mm(out, lhsT, rhs, start, stop):
        ops("pe", lambda e: e.matmul(out.ap, lhsT=lhsT.ap, rhs=rhs.ap, start=start, stop=stop), reads=[lhsT, rhs], writes=[out])

    def transpose(out, in_, ident):
        ops("pe", lambda e: e.transpose(out.ap, in_.ap, ident.ap), reads=[in_, ident], writes=[out])

    st = {"next_load": 0, "done": -1, "tile": 0, "gu": 0}

    def unit_src(u):
        name, kcn, c0, ncols = units[u]
        return W[name][:, c0:c0 + ncols].rearrange("(kc p) n -> p kc n", p=128), kcn, ncols

    def emit_load(gu):
        tile, u = divmod(gu, NU)
        slot = gu % RING
        src, kcn, ncols = unit_src(u)
        ne = kcn * ncols
        dst = ring[slot][:, 0:ne]
        if tile == 0:
            nparts = max(1, kcn // 8)
            kp = kcn // nparts
            for pi in range(nparts):
                dpart = ring[slot][:, pi * kp * ncols:(pi + 1) * kp * ncols]
                spart = src[:, pi * kp:(pi + 1) * kp, :]
                ops("pool", lambda e, dpart=dpart, spart=spart: e.dma_start(out=dpart.ap.rearrange("p (k n) -> p k n", k=kp), in_=spart),
                    writes=[dpart], dma="wlp%d_%d" % (slot, pi))
            sc = wscr_b[u].acc(wscr[u, :, 0:ne])
            ops("sp", lambda e: e.dma_start(out=sc.ap, in_=dst.ap), reads=[dst], writes=[sc], dma="ws%d" % slot)
        else:
            sc = wscr_b[u].acc(wscr[u, :, 0:ne])
            ops("sp", lambda e: e.dma_start(out=dst.ap, in_=sc.ap), reads=[sc], writes=[dst], dma="wl%d" % slot)

    def pump():
        total = ntiles * NU
        while st["next_load"] < total and st["next_load"] <= st["done"] + RING:
            emit_load(st["next_load"])
            st["next_load"] += 1

    class UnitView:
        def __init__(self, gu):
            name, kcn, c0, ncols = units[gu % NU]
            self.t = ring[gu % RING]
            self.kcn, self.ncols = kcn, ncols

        def sl(self, kc, a, b):
            o = kc * self.ncols
            return self.t[:, o + a:o + b]

    def take(k):
        g0 = st["gu"]
        while st["next_load"] < g0 + k:
            assert st["next_load"] <= st["done"] + RING, "ring too small for group"
            emit_load(st["next_load"])
            st["next_load"] += 1
        st["gu"] = g0 + k
        return [UnitView(g0 + i) for i in range(k)]

    def release():
        st["done"] = st["gu"] - 1
        pump()

    ops("sp", lambda e: e.dma_start(out=cpp[:].ap, in_=cpp_d), writes=[cpp[:]], dma="c0")
    ops("sp", lambda e: e.dma_start(out=bcs[:].ap, in_=bcs_d[:, 0:96]), writes=[bcs[:]], dma="c1")
    ops("sp", lambda e: e.dma_start(out=t1[:, 0:1024].ap, in_=bcs_d[:, 96:96 + 1024]), writes=[t1[:, 0:1024]], dma="c2")
    ops("sp", lambda e: e.dma_start(out=t2[:, 0:1024].ap, in_=wst_d.rearrange("p g t -> p (g t)")), writes=[t2[:, 0:1024]], dma="c3")
    ops("pool", lambda e: e.memset(ones_f[:].ap, 1.0), writes=[ones_f[:]])
    ops("pool", lambda e: e.memset(ones_b[:].ap, 1.0), writes=[ones_b[:]])

    def mask(dst, pat, cm, cmp):
        ops("pool", lambda e: e.affine_select(out=dst[:].ap, in_=ones_f[:].ap, pattern=[[pat, 128]], compare_op=cmp, fill=0.0, base=0,
                                              channel_multiplier=cm), reads=[ones_f[:]], writes=[dst[:]])
    mask(tri_b, 1, -1, ALU.is_ge)
    mask(tri_f, 1, -1, ALU.is_ge)
    mask(strict_b, -1, 1, ALU.is_gt)
    mask(ident_b, 1, -1, ALU.is_equal)
    mask(ident_f, 1, -1, ALU.is_equal)
    ops("act", lambda e: e.activation(out=bcs[:, B_ALOG:B_ALOG + 32].ap, in_=bcs[:, B_ALOG:B_ALOG + 32].ap, func=AF.Exp),
        reads=[bcs[:, B_ALOG:B_ALOG + 32]], writes=[bcs[:, B_ALOG:B_ALOG + 32]])
    ops("dve", lambda e: e.tensor_scalar(out=bcs[:, B_ALOG:B_ALOG + 32].ap, in0=bcs[:, B_ALOG:B_ALOG + 32].ap, scalar1=-1.0, scalar2=None, op0=ALU.mult),
        reads=[bcs[:, B_ALOG:B_ALOG + 32]], writes=[bcs[:, B_ALOG:B_ALOG + 32]])
    t2v = t2[:, 0:1024]
    ops("dve", lambda e: e.tensor_copy(WsT[:].ap, t2v.ap.rearrange("p (g t) -> p g t", g=8)), reads=[t2v], writes=[WsT[:]])
    ops("dve", lambda e: e.memset(WsT[64:128, :, 0:64].ap, 0.0), writes=[WsT[:]])
    pw = PSB(2, F32, [8, 128])
    for hh in range(2):
        mm(pw[:, 4 * hh:4 * hh + 4, :], ones_b[:], WsT[:, 4 * hh:4 * hh + 4, :], True, True)
    for g in range(8):
        ops("dve", lambda e, g=g: e.scalar_tensor_tensor(out=Kc[:, g, :].ap, in0=pw[:, g, :].ap, scalar=cpp[:, P_LNB + g:P_LNB + g + 1].ap,
                                                         in1=t1[:, 128 * g:128 * g + 128].ap, op0=ALU.mult, op1=ALU.add),
            reads=[pw[:, g, :], cpp[:], t1[:, 128 * g:128 * g + 128]], writes=[Kc[:, g, :]])

    pump()
    try:
        chk(0)
    except _Stop:
        DEBUG_STOP[0] = -1

    NORM_BANK = 7
    nst = {"cnt": 0}

    pend = []

    def norm_acc(kc, cols=None, defer=True):
        ps = PSX(NORM_BANK)
        c0, c1 = cols if cols is not None else (0, TT)
        sq = sqb[kc % 2]
        norm_flush()
        ops("act", lambda e: e.activation(out=sq[:, c0:c1].ap, in_=h[:, kc, c0:c1].ap, func=AF.Square), reads=[h[:, kc, c0:c1]], writes=[sq[:, c0:c1]])
        pend.append(lambda: mm(ps[:, c0:c1], ones_b[:], sq[:, c0:c1], kc == 0, kc == KC - 1))
        if not defer:
            norm_flush()

    def norm_flush():
        while pend:
            pend.pop(0)()

    def norm_finish(col, dst_is_h=False):
        norm_flush()
        ps = PSX(NORM_BANK)
        ops("act", lambda e: e.activation(out=lnt[:].ap, in_=ps[:].ap, func=AF.Ln, scale=1.0 / D, bias=EPS), reads=[ps[:]], writes=[lnt[:]])
        ops("act", lambda e: e.activation(out=rstd[:].ap, in_=lnt[:].ap, func=AF.Exp, scale=-0.5), reads=[lnt[:]], writes=[rstd[:]])
        for kc in range(KC):
            dst = hout[:, kc, :] if dst_is_h else n[:, kc, :]
            ops("dve", lambda e, kc=kc, dst=dst: e.scalar_tensor_tensor(out=dst.ap, in0=h[:, kc, :].ap, scalar=cpp[:, col + kc:col + kc + 1].ap, in1=rstd[:].ap,
                                                                        op0=ALU.mult, op1=ALU.mult),
                reads=[h[:, kc, :], cpp[:], rstd[:]], writes=[dst])

    def ffn(col):
        norm_finish(col)
        chk(2)
        for i in range(8):
            Ug, Uu = take(2)
            for jj in range(4):
                j = 4 * i + jj
                pg = PSB()
                for kc in range(KC):
                    mm(pg[:], Ug.sl(kc, 128 * jj, 128 * jj + 128), n[:, kc, :], kc == 0, kc == KC - 1)
                pu = PSB()
                for kc in range(KC):
                    mm(pu[:], Uu.sl(kc, 128 * jj, 128 * jj + 128), n[:, kc, :], kc == 0, kc == KC - 1)
                s_ = sg[j % 2]
                ops("act", lambda e, pg=pg, s_=s_: e.activation(out=s_[:].ap, in_=pg[:].ap, func=AF.Silu), reads=[pg[:]], writes=[s_[:]])
                ops("dve", lambda e, pu=pu, s_=s_, j=j: e.tensor_tensor(out=act[:, j, :].ap, in0=pu[:].ap, in1=s_[:].ap, op=ALU.mult),
                    reads=[pu[:], s_[:]], writes=[act[:, j, :]])
            release()
            flush_store(1)
        for m in range(8):
            (U,) = take(1)
            po = PSB()
            for kc in range(32):
                mm(po[:], U.sl(kc, 0, 128), act[:, kc, :], kc == 0, kc == 31)
            ops("dve", lambda e, po=po, m=m: e.scalar_tensor_tensor(out=h[:, m, :].ap, in0=po[:].ap, scalar=0.5, in1=h[:, m, :].ap, op0=ALU.mult, op1=ALU.add),
                reads=[po[:], h[:, m, :]], writes=[h[:, m, :]])
            norm_acc(m)
            release()

    prefetched = set()

    def load_dma(tile, b):
        tok0 = tile * TT
        xi = xin[b % 2]
        ops("sp", lambda e: e.dma_start(out=xi[:].ap, in_=x_d[tok0 + 128 * b:tok0 + 128 * b + 128, :]), writes=[xi[:]], dma="xi%d" % (b % 2))

    def prefetch_x(tile):
        for b in range(2):
            load_dma(tile, b)
            prefetched.add((tile, b))

    def load_blk(tile, b):
        xi = xin[b % 2]
        if (tile, b) not in prefetched:
            load_dma(tile, b)
        for hh in range(2):
            ps = PSB(1, F32, [4, 128])
            for q in range(4):
                kc = 4 * hh + q
                transpose(ps[:, q, :], xi[:, 128 * kc:128 * kc + 128], ident_f[:])
            copy_op(alt(), h[:, 4 * hh:4 * hh + 4, 128 * b:128 * b + 128], ps[:])
        for kc in range(KC):
            norm_acc(kc, (128 * b, 128 * b + 128))

    def store_blk(tile, b):
        tok0 = tile * TT
        o = ost[b % 2]
        for hh in range(2):
            ps = PSB()
            for q in range(4):
                kc = 4 * hh + q
                transpose(ps[:, 128 * q:128 * q + 128], hout[:, kc, 128 * b:128 * b + 128], ident_f[:])
            copy_op(alt(), o[:, 512 * hh:512 * hh + 512], ps[:])
        od = out_b[b % 2].acc(out_d[tok0 + 128 * b:tok0 + 128 * b + 128, :])
        ops("sp", lambda e: e.dma_start(out=od.ap, in_=o[:].ap), reads=[o[:]], writes=[od], dma="oo%d" % (b % 2))

    pending_store = []

    def boundary(tile_done, tile_next):
        if tile_done is not None:
            norm_finish(P_FIN, dst_is_h=True)
            for b in range(NBLK):
                pending_store.append((tile_done, b))
        if tile_next is not None:
            for b in range(NBLK):
                load_blk(tile_next, b)
        else:
            flush_store()

    def flush_store(k=None):
        cnt_ = 0
        while pending_store and (k is None or cnt_ < k):
            td, b = pending_store.pop(0)
            store_blk(td, b)
            cnt_ += 1

    def bc3(acc_, shape, axis):
        return acc_.ap.unsqueeze(axis).to_broadcast(shape)

    def mixer(first_of_seq):
        if first_of_seq:
            ops("pool", lambda e: e.memset(Sst[:].ap, 0.0), writes=[Sst[:]])
            ops("pool", lambda e: e.memset(S_bf[:].ap, 0.0), writes=[S_bf[:]])
            ops("pool", lambda e: e.memset(halo[:].ap, 0.0), writes=[halo[:]])
        norm_finish(P_MIX)
        Us = take(2)
        for m in range(8):
            U = Us[m // 4]
            ps = PSB()
            for kc in range(KC):
                mm(ps[:], U.sl(kc, 128 * (m % 4), 128 * (m % 4) + 128), n[:, kc, :], kc == 0, kc == KC - 1)
            ops("act", lambda e, ps=ps, m=m: e.activation(out=ugl[:, m, :].ap, in_=ps[:].ap, func=AF.Gelu), reads=[ps[:]], writes=[ugl[:, m, :]])
        release()
        chk(4)
        Vs = take(2)
        for b in range(NBLK):
            vgb = vg[b % 2]
            for hh in range(2):
                ps = PSB()
                for kc in range(KC):
                    mm(ps[:], n[:, kc, 128 * b:128 * b + 128], Vs[hh].sl(kc, 0, 512), kc == 0, kc == KC - 1)
                ops("act", lambda e, ps=ps, vgb=vgb, hh=hh: e.activation(out=vgb[:, 512 * hh:512 * hh + 512].ap, in_=ps[:].ap, func=AF.Gelu),
                    reads=[ps[:]], writes=[vgb[:, 512 * hh:512 * hh + 512]])
                ops("dve", lambda e, vgb=vgb, hh=hh: e.bn_stats(stt[:, hh, :].ap, vgb[:, 512 * hh:512 * hh + 512].ap),
                    reads=[vgb[:, 512 * hh:512 * hh + 512]], writes=[stt[:, hh, :]])
            ops("dve", lambda e: e.bn_aggr(mv[:, 0:2].ap, stt[:].ap), reads=[stt[:]], writes=[mv[:, 0:2]])
            ops("act", lambda e: e.activation(out=mv[:, 2:3].ap, in_=mv[:, 1:2].ap, func=AF.Ln, bias=EPS), reads=[mv[:, 1:2]], writes=[mv[:, 2:3]])
            ops("act", lambda e: e.activation(out=mv[:, 3:4].ap, in_=mv[:, 2:3].ap, func=AF.Exp, scale=-0.5), reads=[mv[:, 2:3]], writes=[mv[:, 3:4]])
            ops("dve", lambda e, vgb=vgb, b=b: e.tensor_scalar(out=v_ln[:, b, :].ap, in0=vgb[:].ap, scalar1=mv[:, 0:1].ap, scalar2=mv[:, 3:4].ap,
                                                                op0=ALU.subtract, op1=ALU.mult),
                reads=[vgb[:], mv[:]], writes=[v_ln[:, b, :]])
        release()
        chk(7)
        for q in range(4):
            (Z,) = take(1)
            for b in range(NBLK):
                ps = PSB()
                for kc in range(KC):
                    mm(ps[:], n[:, kc, 128 * b:128 * b + 128], Z.sl(kc, 0, 512), kc == 0, kc == KC - 1)
                ops("act", lambda e, ps=ps, b=b, q=q: e.activation(out=zs[:, b, 512 * q:512 * q + 512].ap, in_=ps[:].ap, func=AF.Silu),
                    reads=[ps[:]], writes=[zs[:, b, 512 * q:512 * q + 512]])
            release()
        chk(5)
        for b in range(NBLK):
            for hh in range(2):
                ps = PSB(1, F32, [4, 128])
                for gg in range(4):
                    g = 4 * hh + gg
                    mm(ps[:, gg, :], v_ln[:, b, 128 * g:128 * g + 128], WsT[:, g, :], True, True)
                for gg in range(4):
                    g = 4 * hh + gg
                    ops("dve", lambda e, ps=ps, gg=gg, g=g: e.scalar_tensor_tensor(out=ftmp[:, gg, :].ap, in0=ps[:, gg, :].ap,
                                                                                    scalar=cpp[:, P_LNG + g:P_LNG + g + 1].ap, in1=Kc[:, g, :].ap,
                                                                                    op0=ALU.mult, op1=ALU.add),
                        reads=[ps[:, gg, :], cpp[:], Kc[:, g, :]], writes=[ftmp[:, gg, :]])
                dst = ugl[:, 4 * hh:4 * hh + 4, 128 * b:128 * b + 128]
                ops("pool", lambda e, dst=dst: e.tensor_tensor(out=dst.ap, in0=dst.ap, in1=ftmp[:].ap, op=ALU.mult), reads=[dst, ftmp[:]], writes=[dst])
        chk(6)
        for hh in range(2):
            GA, WA = take(2)
            for mm_ in range(4):
                m = 4 * hh + mm_
                p1 = PSB()
                for kc in range(KC):
                    mm(p1[:], GA.sl(kc, 128 * mm_, 128 * mm_ + 128), n[:, kc, :], kc == 0, kc == KC - 1)
                p2 = PSB()
                for kc in range(KC):
                    mm(p2[:], WA.sl(kc, 128 * mm_, 128 * mm_ + 128), ugl[:, kc, :], kc == 0, kc == KC - 1)
                s_ = sgt[m % 2]
                ops("act", lambda e, p1=p1, s_=s_: e.activation(out=s_[:].ap, in_=p1[:].ap, func=AF.Sigmoid), reads=[p1[:]], writes=[s_[:]])
                ops("dve", lambda e, p2=p2, s_=s_, m=m: e.tensor_tensor(out=merged[:, m, :].ap, in0=p2[:].ap, in1=s_[:].ap, op=ALU.mult),
                    reads=[p2[:], s_[:]], writes=[merged[:, m, :]])
            release()
        chk(8)
        xunit = {}

        def xbc_A(sidx):
            i, sgp = divmod(sidx, 2)
            if sgp == 0:
                (xunit["X"],) = take(1)
            X = xunit["X"]
            cc0 = 2 * sidx
            ps2 = PSX(2 * (sidx % 3), 2, F32, [2, 512])
            for c2 in range(2):
                jj = 2 * sgp + c2
                for kc in range(KC):
                    mm(ps2[:, c2, :], X.sl(kc, 128 * jj, 128 * jj + 128), n[:, kc, :], kc == 0, kc == KC - 1)
            if sgp == 1:
                release()
            ac2 = acc2[sidx % 3]
            for c2 in range(2):
                cc = cc0 + c2
                ops("act", lambda e, c2=c2, cc=cc: e.activation(out=ac2[:, c2, :].ap, in_=ps2[:, c2, :].ap, func=AF.Identity,
                                                               scale=cpp[:, P_CW + 4 * cc + 3:P_CW + 4 * cc + 4].ap,
                                                               bias=cpp[:, P_CB + cc:P_CB + cc + 1].ap),
                    reads=[ps2[:, c2, :], cpp[:]], writes=[ac2[:, c2, :]])
            return ps2, ac2

        def xbc_B(sidx, ps2, ac2):
            cc0 = 2 * sidx
            for k in (2, 1, 0):
                sft = 3 - k
                for c2 in range(2):
                    cc = cc0 + c2
                    ops("dve", lambda e, c2=c2, cc=cc, k=k, sft=sft: e.scalar_tensor_tensor(
                        out=ac2[:, c2, sft:512].ap, in0=ps2[:, c2, 0:512 - sft].ap, scalar=cpp[:, P_CW + 4 * cc + k:P_CW + 4 * cc + k + 1].ap,
                        in1=ac2[:, c2, sft:512].ap, op0=ALU.mult, op1=ALU.add),
                        reads=[ps2[:, c2, :], cpp[:], ac2[:, c2, :]], writes=[ac2[:, c2, :]])
            ops("dve", lambda e: e.tensor_tensor(out=ac2[:, :, 0:3].ap, in0=ac2[:, :, 0:3].ap, in1=hcb[:, cc0:cc0 + 2, :].ap, op=ALU.add),
                reads=[ac2[:, 0, 0:4], ac2[:, 1, 0:4], hcb[:]], writes=[ac2[:, 0, 0:4], ac2[:, 1, 0:4]])
            ops("dve", lambda e: e.tensor_copy(halo[:, cc0:cc0 + 2, :].ap, ps2[:, :, 509:512].ap), reads=[ps2[:]], writes=[halo[:, cc0:cc0 + 2, :]])

        def xbc_C(sidx, ac2):
            cc0 = 2 * sidx
            xo_ = xo2[sidx % 2]
            for c2 in range(2):
                cc = cc0 + c2
                if cc < 16:
                    dst = xo_[:, c2, :]
                elif cc < 24:
                    dst = BT[:, cc - 16, :]
                else:
                    dst = CT[:, cc - 24, :]
                ops("act", lambda e, c2=c2, dst=dst: e.activation(out=dst.ap, in_=ac2[:, c2, :].ap, func=AF.Silu), reads=[ac2[:, c2, :]], writes=[dst])
            if cc0 < 24:
                pt = PSX(6 + sidx % 2, 1, BF16, [4, 2, 128])
                for c2 in range(2):
                    cc = cc0 + c2
                    for b in range(NBLK):
                        srcb = (xo_[:, c2, 128 * b:128 * b + 128] if cc < 16 else BT[:, cc - 16, 128 * b:128 * b + 128])
                        transpose(pt[:, b, c2, :], srcb, ident_b[:])
                if cc0 < 16:
                    dstt = xs_tok[:, :, 128 * cc0:128 * cc0 + 256]
                else:
                    dstt = B_tok[:, :, 128 * (cc0 - 16):128 * (cc0 - 16) + 256]
                ptf = PSX(6 + sidx % 2, 1, BF16, [4, 256])
                copy_op(alt(), dstt, ptf[:])

        cw3 = cpp[:, P_CW:P_CW + 128]
        for k in (0, 1, 2):
            w_ = 3 - k
            dstk = hcb[:, :, 0:w_] if k == 0 else hct[:, :, 0:w_]
            ops("dve", lambda e, k=k, w_=w_, dstk=dstk: e.tensor_tensor(
                out=dstk.ap, in0=halo[:, :, k:3].ap,
                in1=cw3.ap.rearrange("p (j q) -> p j q", q=4)[:, :, k:k + 1].to_broadcast([128, 32, w_]), op=ALU.mult),
                reads=[halo[:], cpp[:]], writes=[dstk])
            if k > 0:
                ops("dve", lambda e, w_=w_: e.tensor_tensor(out=hcb[:, :, 0:w_].ap, in0=hcb[:, :, 0:w_].ap, in1=hct[:, :, 0:w_].ap, op=ALU.add),
                    reads=[hcb[:], hct[:]], writes=[hcb[:]])
        stA = {0: xbc_A(0)}
        for sidx in range(16):
            if sidx + 1 < 16:
                stA[sidx + 1] = xbc_A(sidx + 1)
            ps2_, ac2_ = stA.pop(sidx)
            xbc_B(sidx, ps2_, ac2_)
            xbc_C(sidx, ac2_)
        chk(9)
        (DT,) = take(1)
        pd = PSB(1, F32, [16, 32])
        for b in range(NBLK):
            for kc in range(KC):
                mm(pd[:, b, :], n[:, kc, 128 * b:128 * b + 128], DT.sl(kc, 0, 32), kc == 0, kc == KC - 1)
        ops("dve", lambda e: e.tensor_tensor(out=xdt_all[:].ap, in0=pd[:, 0:4, :].ap, in1=bc3(bcs[:, B_DTB:B_DTB + 32], [128, 4, 32], 1), op=ALU.add),
            reads=[pd[:, 0:4, :], bcs[:]], writes=[xdt_all[:]])
        release()
        chk(10)
        ssd_tile()
        chk(12)
        for hh in range(2):
            GB, WB0, WB1 = take(3)
            for mm_ in range(4):
                m = 4 * hh + mm_
                WB = (WB0, WB1)[mm_ // 2]
                p1 = PSB()
                for kc in range(KC):
                    mm(p1[:], GB.sl(kc, 128 * mm_, 128 * mm_ + 128), n[:, kc, :], kc == 0, kc == KC - 1)
                p2 = PSB()
                for kc in range(16):
                    mm(p2[:], WB.sl(kc, 128 * (mm_ % 2), 128 * (mm_ % 2) + 128), ygT[:, kc, :], kc == 0, kc == 15)
                s_ = sgt[m % 2]
                g_ = sg[m % 2]
                ops("act", lambda e, p1=p1, s_=s_: e.activation(out=s_[:].ap, in_=p1[:].ap, func=AF.Sigmoid), reads=[p1[:]], writes=[s_[:]])
                ops("dve", lambda e, p2=p2, s_=s_, g_=g_: e.tensor_tensor(out=g_[:].ap, in0=p2[:].ap, in1=s_[:].ap, op=ALU.mult),
                    reads=[p2[:], s_[:]], writes=[g_[:]])
                ops("pool", lambda e, g_=g_, m=m: e.tensor_tensor(out=merged[:, m, :].ap, in0=merged[:, m, :].ap, in1=g_[:].ap, op=ALU.add),
                    reads=[merged[:, m, :], g_[:]], writes=[merged[:, m, :]])
            release()
        chk(13)
        WOs = take(2)
        for m in range(8):
            WO = WOs[m // 4]
            ps = PSB()
            for kc in range(KC):
                mm(ps[:], WO.sl(kc, 128 * (m % 4), 128 * (m % 4) + 128), merged[:, kc, :], kc == 0, kc == KC - 1)
            ops("dve", lambda e, ps=ps, m=m: e.tensor_tensor(out=h[:, m, :].ap, in0=ps[:].ap, in1=h[:, m, :].ap, op=ALU.add),
                reads=[ps[:], h[:, m, :]], writes=[h[:, m, :]])
            norm_acc(m)
        release()

    def ssd_tile():
        def smb(b):
            sm_t = small2[b % 2]
            return lambda i, w=32: sm_t[:, i, 0:w]

        def H1(b):
            sm = smb(b)
            sm_t = small2[b % 2]
            blk = slice(128 * b, 128 * b + 128)
            xdt_b = xdt_all[:, b, :]
            ops("dve", lambda e: e.scalar_tensor_tensor(out=sm(0).ap, in0=xdt_b.ap, scalar=-1.0, in1=xdt_b.ap, op0=ALU.mult, op1=ALU.max), reads=[xdt_b], writes=[sm(0)])
            ops("act", lambda e: e.activation(out=sm(1).ap, in_=sm(0).ap, func=AF.Exp, scale=-1.0), reads=[sm(0)], writes=[sm(1)])
            ops("act", lambda e: e.activation(out=sm(2).ap, in_=sm(1).ap, func=AF.Ln, bias=1.0), reads=[sm(1)], writes=[sm(2)])
            ops("dve", lambda e: e.scalar_tensor_tensor(out=sm(3).ap, in0=xdt_b.ap, scalar=0.0, in1=sm(2).ap, op0=ALU.max, op1=ALU.add),
                reads=[xdt_b, sm(2)], writes=[sm(3)])
            ops("dve", lambda e: e.tensor_tensor(out=sm(4).ap, in0=sm(3).ap, in1=bcs[:, B_ALOG:B_ALOG + 32].ap, op=ALU.mult), reads=[sm(3), bcs[:]], writes=[sm(4)])
            pc = PSX(4, 1, F32, [16, 32])
            mm(pc[:, 0, :], tri_f[:], sm(4), True, True)
            mm(pc[:, 1, :], ones_f[:], sm(4), True, True)
            ops("act", lambda e: e.activation(out=sm_t[:, 5:7, :].ap, in_=pc[:, 0:2, :].ap, func=AF.Exp), reads=[pc[:, 0:2, :]], writes=[sm_t[:, 5:7, :]])
            ops("dve", lambda e: e.tensor_copy(sm(10).ap, pc[:, 0, :].ap), reads=[pc[:, 0:2, :]], writes=[sm(10)])
            ops("dve", lambda e: e.tensor_tensor(out=sm(9).ap, in0=pc[:, 1, :].ap, in1=sm(10).ap, op=ALU.subtract), reads=[pc[:, 0:2, :], sm(10)], writes=[sm(9)])
            ops("act", lambda e: e.activation(out=sm(7).ap, in_=sm(9).ap, func=AF.Exp), reads=[sm(9)], writes=[sm(7)])
            xs_b = xs_tok[:, b, :]
            ops("dve", lambda e: e.tensor_tensor(out=xdt_t[:].ap.rearrange("p (k j) -> p k j", k=32), in0=xs_b.ap.rearrange("p (k j) -> p k j", k=32),
                                                 in1=bc3(sm(3), [128, 32, 64], 2), op=ALU.mult), reads=[xs_b, sm(3)], writes=[xdt_t[:]])
            ops("dve", lambda e: e.tensor_tensor(out=xd[:].ap.rearrange("p (k j) -> p k j", k=32), in0=xdt_t[:].ap.rearrange("p (k j) -> p k j", k=32),
                                                  in1=bc3(sm(7), [128, 32, 64], 2), op=ALU.mult), reads=[xdt_t[:], sm(7)], writes=[xd[:]])
            pcb = PSX(6, 2, F32, [8, 128])
            for g in range(8):
                mm(pcb[:, g, :], BT[:, g, blk], CT[:, g, blk], True, True)
            ops("dve", lambda e: e.tensor_tensor(out=cbm[:].ap, in0=pcb[:].ap, in1=bc3(tri_b[:], [128, 8, 128], 1), op=ALU.mult),
                reads=[pcb[:], tri_b[:]], writes=[cbm[:]])

        def H2(b):
            sm_t = small2[b % 2]
            blk = slice(128 * b, 128 * b + 128)
            pyo_f = PSX(4, 4, F32)
            for g in range(8):
                mm(pyo_f[:, 256 * g:256 * g + 256], CT[:, g, blk], S_bf[:, 256 * g:256 * g + 256], True, True)
            for hf in range(2):
                c0, c1 = 1024 * hf, 1024 * hf + 1024
                pyo_h = PSX(4 + 2 * hf, 2, F32, [16, 64])
                ops("dve", lambda e, c0=c0, c1=c1, pyo_h=pyo_h, hf=hf: e.tensor_tensor(out=t1[:, c0:c1].ap.rearrange("p (k j) -> p k j", k=16), in0=pyo_h[:].ap,
                                                                                 in1=bc3(sm_t[:, 5, 16 * hf:16 * hf + 16], [128, 16, 64], 2), op=ALU.mult),
                    reads=[pyo_h[:], sm_t[:, 5, 0:32]], writes=[t1[:, c0:c1]])
                ops("dve", lambda e, c0=c0, c1=c1, hf=hf: e.tensor_tensor(out=t2[:, c0:c1].ap.rearrange("p (k j) -> p k j", k=16),
                                                                       in0=xs_tok[:, b, c0:c1].ap.rearrange("p (k j) -> p k j", k=16),
                                                                       in1=bc3(bcs[:, B_DSK + 16 * hf:B_DSK + 16 * hf + 16], [128, 16, 64], 2), op=ALU.mult),
                    reads=[xs_tok[:, b, c0:c1], bcs[:]], writes=[t2[:, c0:c1]])

        def U(b):
            sm = smb(b)
            pst = PSX(4, 4, F32)
            for g in range(8):
                mm(pst[:, 256 * g:256 * g + 256], B_tok[:, b, 128 * g:128 * g + 128], xd[:, 256 * g:256 * g + 256], True, True)
            ops("dve", lambda e: e.tensor_tensor(out=Sst[:].ap.rearrange("p (k j) -> p k j", k=32), in0=Sst[:].ap.rearrange("p (k j) -> p k j", k=32),
                                                  in1=bc3(sm(6), [128, 32, 64], 2), op=ALU.mult), reads=[Sst[:], sm(6)], writes=[Sst[:]])
            ops("dve", lambda e: e.tensor_tensor(out=Sst[:].ap, in0=pst[:].ap, in1=Sst[:].ap, op=ALU.add), reads=[pst[:], Sst[:]], writes=[Sst[:]])
            ops("act", lambda e: e.activation(out=S_bf[:].ap, in_=Sst[:].ap, func=AF.Copy), reads=[Sst[:]], writes=[S_bf[:]])

        def Q(b):
            sm_t = small2[b % 2]
            pyd = PSX(0, 4, F32)
            st_ = {}

            def R(q):
                def f():
                    R_ = Rq[q % 2]
                    ops("dve", lambda e: e.tensor_tensor(out=R_[:].ap, in0=bc3(tri_b[:], [128, 8, 128], 1),
                                                          in1=bc3(sm_t[:, 4, 8 * q:8 * q + 8], [128, 8, 128], 2), op=ALU.mult),
                        reads=[tri_b[:], sm_t[:, 4, 0:32]], writes=[R_[:]])
                return f

            def seg(q):
                def f():
                    R_ = Rq[q % 2]
                    pseg = PSX(4, 2, F32, [8, 128])
                    st_[q] = pseg
                    for hh in range(2):
                        mm(pseg[:, 4 * hh:4 * hh + 4, :], strict_b[:], R_[:, 4 * hh:4 * hh + 4, :], True, True)
                return f

            def E(q):
                def f():
                    EM_ = EMq[q % 2]
                    pseg = st_[q]
                    ops("act", lambda e: e.activation(out=EM_[:].ap, in_=pseg[:].ap, func=AF.Exp), reads=[pseg[:]], writes=[EM_[:]])
                return f

            def Mq(q):
                def f():
                    EM_ = EMq[q % 2]
                    ops("dve", lambda e: e.tensor_tensor(out=EM_[:].ap.rearrange("p (g k) t -> p g k t", g=2),
                                                         in0=EM_[:].ap.rearrange("p (g k) t -> p g k t", g=2),
                                                         in1=cbm[:, 2 * q:2 * q + 2, :].ap.unsqueeze(2).to_broadcast([128, 2, 4, 128]), op=ALU.mult),
                        reads=[EM_[:], cbm[:, 2 * q:2 * q + 2, :]], writes=[EM_[:]])
                return f

            def yd(q):
                def f():
                    EM_ = EMq[q % 2]
                    for kk in range(8):
                        k = 8 * q + kk
                        mm(pyd[:, 64 * k:64 * k + 64], EM_[:, kk, :], xdt_t[:, 64 * k:64 * k + 64], True, True)
                return f
            return [R(0), R(1), seg(0), E(0), seg(1), Mq(0), E(1), yd(0), R(2), seg(2), Mq(1), E(2), yd(1), R(3), seg(3), Mq(2), E(3), yd(2), Mq(3), yd(3)]

        def T(b):
            sm_t = small2[b % 2]
            blk = slice(128 * b, 128 * b + 128)
            HC = [(0, 1024), (1024, 2048)]
            srow = [8, 12]

            def t0_():
                for hf, (c0, c1) in enumerate(HC):
                    pyd_h = PSX(2 * hf, 2, F32)
                    ops("dve", lambda e, c0=c0, c1=c1, pyd_h=pyd_h: e.tensor_tensor(out=t1[:, c0:c1].ap, in0=pyd_h[:].ap, in1=t1[:, c0:c1].ap, op=ALU.add),
                        reads=[pyd_h[:], t1[:, c0:c1]], writes=[t1[:, c0:c1]])
                    ops("dve", lambda e, c0=c0, c1=c1: e.tensor_tensor(out=t1[:, c0:c1].ap, in0=t1[:, c0:c1].ap, in1=t2[:, c0:c1].ap, op=ALU.add),
                        reads=[t1[:, c0:c1], t2[:, c0:c1]], writes=[t1[:, c0:c1]])

            def t1_():
                for hf, (c0, c1) in enumerate(HC):
                    ops("dve", lambda e, c0=c0, c1=c1: e.tensor_tensor(out=t1[:, c0:c1].ap, in0=t1[:, c0:c1].ap, in1=zs[:, b, c0:c1].ap, op=ALU.mult),
                        reads=[t1[:, c0:c1], zs[:, b, c0:c1]], writes=[t1[:, c0:c1]])

            def t2_():
                for hf, (c0, c1) in enumerate(HC):
                    ops("act", lambda e, c0=c0, c1=c1: e.activation(out=t2[:, c0:c1].ap, in_=t1[:, c0:c1].ap, func=AF.Square), reads=[t1[:, c0:c1]], writes=[t2[:, c0:c1]])

            def t3_():
                for hf, (c0, c1) in enumerate(HC):
                    r = srow[hf]
                    ops("dve", lambda e, c0=c0, c1=c1, r=r: e.tensor_reduce(out=sm_t[:, r, 0:4].ap, in_=t2[:, c0:c1].ap.rearrange("p (g j) -> p g j", g=4), axis=AX.X, op=ALU.add),
                        reads=[t2[:, c0:c1]], writes=[sm_t[:, r, 0:4]])

            def t4_():
                for hf in range(2):
                    r = srow[hf]
                    ops("act", lambda e, r=r: e.activation(out=sm_t[:, r + 1, 0:4].ap, in_=sm_t[:, r, 0:4].ap, func=AF.Ln, scale=1.0 / 256, bias=EPS),
                        reads=[sm_t[:, r, 0:4]], writes=[sm_t[:, r + 1, 0:4]])
                    ops("act", lambda e, r=r: e.activation(out=sm_t[:, r, 0:4].ap, in_=sm_t[:, r + 1, 0:4].ap, func=AF.Exp, scale=-0.5),
                        reads=[sm_t[:, r + 1, 0:4]], writes=[sm_t[:, r, 0:4]])

            def t5_():
                for hf, (c0, c1) in enumerate(HC):
                    r = srow[hf]
                    ops("dve", lambda e, c0=c0, c1=c1, r=r: e.tensor_tensor(out=yg[:, c0:c1].ap.rearrange("p (g j) -> p g j", g=4),
                                                                        in0=t1[:, c0:c1].ap.rearrange("p (g j) -> p g j", g=4),
                                                                        in1=bc3(sm_t[:, r, 0:4], [128, 4, 256], 2), op=ALU.mult),
                        reads=[t1[:, c0:c1], sm_t[:, r, 0:4]], writes=[yg[:, c0:c1]])

            def t6_():
                for hf in range(2):
                    pt = PSX(6 + hf, 1, BF16, [8, 128])
                    for c8 in range(8):
                        cc = 8 * hf + c8
                        transpose(pt[:, c8, :], yg[:, 128 * cc:128 * cc + 128], ident_b[:])
                    dst = ygT[:, 8 * hf:8 * hf + 8, blk]
                    ops("dve", lambda e, pt=pt, dst=dst, hf=hf: e.tensor_tensor(out=dst.ap, in0=pt[:].ap,
                                                                             in1=bc3(cpp[:, P_SSMG + 8 * hf:P_SSMG + 8 * hf + 8], [128, 8, 128], 2), op=ALU.mult),
                        reads=[pt[:], cpp[:]], writes=[dst])
            return [t0_, t1_, t2_, t3_, t4_, t5_, t6_]

        def interleave(A, B):
            A[0]()
            rest = A[1:-1]
            nb, na = len(B), len(rest)
            bi = 0
            for i, ta in enumerate(rest):
                tgt = (i + 1) * nb // (na + 1)
                while bi < tgt:
                    B[bi]()
                    bi += 1
                ta()
            while bi < nb:
                B[bi]()
                bi += 1
            A[-1]()

        H1(0)
        H2(0)
        U(0)
        for f in Q(0):
            f()
        for b in range(NBLK):
            if b + 1 < NBLK:
                H1(b + 1)
                interleave(T(b), Q(b + 1))
                H2(b + 1)
                U(b + 1)
            else:
                for f in T(b):
                    f()

    try:
        boundary(None, 0)
        for tile in range(ntiles):
            if DEBUG_STOP[0] == -1:
                break
            chk(1)
            ffn(P_FFN1)
            chk(3)
            mixer(tile % tiles_per_seq == 0)
            chk(14)
            if tile + 1 < ntiles:
                prefetch_x(tile + 1)
            ffn(P_FFN2)
            chk(15)
            boundary(tile, tile + 1 if tile + 1 < ntiles else None)
    except _Stop:
        pass
    Sched.limit = None
    print("nops", Sched.nops)
    ops("sp", lambda e: e.nop(), reads=[out_b[0].acc(out_d), out_b[1].acc(out_d)])
    with nc.allow_low_precision("bf16 matmul operands, fp32 accumulation"):
        S.emit(nc)
    S.stats["sbuf_used"] = sbuf_used
    return nc, S


def prep_consts(inp):
    f = lambda a: np.ascontiguousarray(np.asarray(a, dtype=np.float32))
    fm = lambda v, nch: f(v).reshape(nch, 128).T
    cpp = np.zeros((128, NCPP), np.float32)
    cpp[:, P_FFN1:P_FFN1 + 8] = fm(inp["ffn1_norm"][0], 8)
    cpp[:, P_MIX:P_MIX + 8] = fm(inp["mix_norm"][0], 8)
    cpp[:, P_FFN2:P_FFN2 + 8] = fm(inp["ffn2_norm"][0], 8)
    cpp[:, P_FIN:P_FIN + 8] = fm(inp["final_norm"], 8)
    cw = f(inp["conv_w"][0])
    cpp[:, P_CW:P_CW + 128] = cw.T.reshape(32, 128, 4).transpose(1, 0, 2).reshape(128, 128)
    cpp[:, P_CB:P_CB + 32] = fm(inp["conv_b"][0], 32)
    cpp[:, P_SSMG:P_SSMG + 16] = fm(inp["ssm_norm"][0], 16)
    cpp[:, P_LNG:P_LNG + 8] = fm(inp["sgu_ln_g"][0], 8)
    cpp[:, P_LNB:P_LNB + 8] = fm(inp["sgu_ln_b"][0], 8)
    bcs = np.zeros((128, NBCS), np.float32)
    bcs[:, B_DTB:B_DTB + 32] = f(inp["dt_bias"][0])[None, :]
    bcs[:, B_ALOG:B_ALOG + 32] = f(inp["a_log"][0])[None, :]
    bcs[:, B_DSK:B_DSK + 32] = f(inp["d_skip"][0])[None, :]
    bcs[:, B_BS:B_BS + 1024] = f(inp["sgu_b_s"][0]).reshape(1, 1024)
    wst = np.ascontiguousarray(f(inp["sgu_w_s"][0]).transpose(2, 0, 1))
    return cpp, bcs, wst


WMAP = {"ffn1_w_in": "ffn1_w_in", "ffn1_w_out": "ffn1_w_out", "w_in": "w_in", "w_a": "w_a", "w_b": "w_b", "w_o": "w_o",
        "ffn2_w_in": "ffn2_w_in", "ffn2_w_out": "ffn2_w_out"}


def make_in_maps(inp, n_cores, n_seq, seq_len):
    cpp, bcs, wst = prep_consts(inp)
    x = np.asarray(inp["x"], dtype=np.float32)
    shared = {k: np.ascontiguousarray(np.asarray(inp[v], dtype=np.float32)[0]) for k, v in WMAP.items()}
    shared.update(cpp=cpp, bcs=bcs, wst=wst)
    maps = []
    for c in range(n_cores):
        xs = np.ascontiguousarray(x[c * n_seq:(c + 1) * n_seq].reshape(n_seq * seq_len, D))
        d = dict(shared)
        d["x"] = xs
        maps.append(d)
    return maps


def kernel(**inputs):
    from concourse.bass_utils import run_bass_kernel_spmd
    x = np.asarray(inputs["x"])
    B, L, _ = x.shape
    n_cores = 8
    n_seq = B // n_cores
    nc, S = build_nc(n_seq, L)
    maps = make_in_maps(inputs, n_cores, n_seq, L)
    res = run_bass_kernel_spmd(nc, maps, core_ids=list(range(n_cores)))
    out = np.stack([r["out"].reshape(n_seq, L, D) for r in res.results], axis=0).reshape(B, L, D)
    return out.astype(np.float32)
```
